# Optimizing a Trainium2 kernel written in Bass

```python
import jax, jax.numpy as jnp
from jax import lax
import numpy as np

D_MODEL = 1024
BATCH = 32
SEQ = 2048
DEPTH = 2

N_A = DEPTH // 2
N_B = DEPTH - N_A
GLA_HEADS = 4
GLA_DK = D_MODEL // 2 // GLA_HEADS
GLA_DV = D_MODEL // GLA_HEADS
GLA_GATE_RANK = 16
GLA_GATE_TAU = 16.0
GLA_CHUNK = 64
GLA_IN = 2 * GLA_HEADS * GLA_DK + 2 * GLA_HEADS * GLA_DV + GLA_GATE_RANK
FOX_HEADS = 16
FOX_HD = D_MODEL // FOX_HEADS
Q_BLOCK = 128
KV_IN = 2 * D_MODEL + FOX_HEADS
D_FF = 2816
CONV_W = 3
PLE_DIM = 256
EPS = 1e-6

kernel_name = "yoco_gla_fox_convffn_ple"


def _rmsnorm(x, g):
    xf = x.astype(jnp.float32)
    y = xf * lax.rsqrt(jnp.mean(xf * xf, axis=-1, keepdims=True) + EPS)
    return (y * g.astype(jnp.float32)).astype(x.dtype)


def _gla_mixer(h, w_in, w_gk2, b_gk2, g_onorm, w_out):
    B, S, _ = h.shape
    H, DK, DV, C = GLA_HEADS, GLA_DK, GLA_DV, GLA_CHUNK
    nc = S // C
    proj = h @ w_in
    q, k, v, og, gr = jnp.split(proj, [H * DK, 2 * H * DK, 2 * H * DK + H * DV,
                                       2 * H * DK + 2 * H * DV], axis=-1)
    gk = jax.nn.log_sigmoid((gr @ w_gk2 + b_gk2).astype(jnp.float32)) / GLA_GATE_TAU

    def chunks(t, d):
        return t.reshape(B, nc, C, H, d).transpose(0, 3, 1, 2, 4)

    q = chunks(q, DK) * (DK ** -0.5)
    k = chunks(k, DK)
    v = chunks(v, DV)
    bcum = jnp.cumsum(chunks(gk, DK), axis=3)
    b_last = bcum[:, :, :, -1:, :]
    q_in = q * jnp.exp(bcum)
    k_in = k * jnp.exp(-bcum)
    k_end = k * jnp.exp(b_last - bcum)
    decay = jnp.exp(b_last[:, :, :, 0, :])

    att = jnp.einsum('bhncd,bhnsd->bhncs', q_in, k_in)
    causal = jnp.tril(jnp.ones((C, C), dtype=bool))
    att = jnp.where(causal, att, 0.0)
    o_intra = jnp.einsum('bhncs,bhnsv->bhncv', att, v)

    def step(state, xs):
        q_n, k_n, v_n, dec_n = xs
        o_n = jnp.einsum('bhcd,bhdv->bhcv', q_n, state)
        state = dec_n[..., None] * state + jnp.einsum('bhcd,bhcv->bhdv', k_n, v_n)
        return state, o_n

    s0 = jnp.zeros((B, H, DK, DV), jnp.float32)
    xs = (jnp.moveaxis(q_in, 2, 0), jnp.moveaxis(k_end, 2, 0),
          jnp.moveaxis(v, 2, 0).astype(jnp.float32), jnp.moveaxis(decay, 2, 0))
    _, o_inter = lax.scan(step, s0, xs)
    o = o_intra + jnp.moveaxis(o_inter, 0, 2)
    o = o.transpose(0, 2, 3, 1, 4).reshape(B, S, H, DV).astype(h.dtype)
    o = _rmsnorm(o, g_onorm).reshape(B, S, H * DV) * jax.nn.silu(og)
    return o @ w_out


def _shared_kv(x, g_norm, w_in, g_knorm, b_f):
    B, S, _ = x.shape
    H, hd = FOX_HEADS, FOX_HD
    h = _rmsnorm(x, g_norm)
    k, v, f = jnp.split(h @ w_in, [D_MODEL, 2 * D_MODEL], axis=-1)
    k = _rmsnorm(k.reshape(B, S, H, hd), g_knorm).transpose(0, 2, 1, 3)
    v = v.reshape(B, S, H, hd).transpose(0, 2, 1, 3)
    log_f = jax.nn.log_sigmoid(f.astype(jnp.float32) + b_f.astype(jnp.float32))
    c = jnp.cumsum(log_f, axis=1).transpose(0, 2, 1)
    return k, v, c


def _fox_mixer(h, w_in, g_qnorm, w_out, k, v, c):
    B, S, _ = h.shape
    H, hd = FOX_HEADS, FOX_HD
    q, og = jnp.split(h @ w_in, [D_MODEL], axis=-1)
    q = _rmsnorm(q.reshape(B, S, H, hd), g_qnorm).transpose(0, 2, 1, 3) * (hd ** -0.5)
    outs = []
    for i in range(S // Q_BLOCK):
        lo, hi = i * Q_BLOCK, (i + 1) * Q_BLOCK
        qb = q[:, :, lo:hi]
        kb = k[:, :, :hi]
        vb = v[:, :, :hi]
        logits = (jnp.einsum('bhqd,bhkd->bhqk', qb, kb).astype(jnp.float32)
                  + c[:, :, lo:hi, None] - c[:, :, None, :hi])
        mask = (lo + jnp.arange(Q_BLOCK))[:, None] >= jnp.arange(hi)[None, :]
        logits = jnp.where(mask, logits, -jnp.inf)
        probs = jax.nn.softmax(logits, axis=-1).astype(vb.dtype)
        outs.append(jnp.einsum('bhqk,bhkd->bhqd', probs, vb))
    o = jnp.concatenate(outs, axis=2).transpose(0, 2, 1, 3).reshape(B, S, D_MODEL)
    return (o * jax.nn.sigmoid(og)) @ w_out


def _conv_ffn(h, w_up, w_dw, b_dw, w_down):
    S = h.shape[1]
    u = h @ w_up
    up = jnp.pad(u, ((0, 0), (CONV_W - 1, 0), (0, 0)))
    u = sum(w_dw[j] * up[:, j:j + S] for j in range(CONV_W)) + b_dw
    a, b = jnp.split(u, [D_FF], axis=-1)
    return (jax.nn.silu(a) * b) @ w_down


def _ple(x, p_i, g, w_gate, b_gate, w_proj):
    gate = jax.nn.sigmoid(_rmsnorm(x, g) @ w_gate + b_gate)
    return (p_i @ w_proj) * gate


def setup_inputs(seed: int = 0) -> dict:
    key = jax.random.key(seed)
    ks = iter(jax.random.split(key, 32))

    def nrm(shape, scale):
        return jax.random.normal(next(ks), shape, jnp.float32) * scale

    def gain(shape):
        return 1.0 + 0.05 * jax.random.normal(next(ks), shape, jnp.float32)

    D, H = D_MODEL, FOX_HEADS
    return {
        "x": nrm((BATCH, SEQ, D), 1.0),
        "p": nrm((DEPTH, BATCH, SEQ, PLE_DIM), 1.0),
        "g_mix": gain((DEPTH, D)),
        "g_ffn": gain((DEPTH, D)),
        "g_ple": gain((DEPTH, D)),
        "gla_w_in": nrm((N_A, D, GLA_IN), D ** -0.5),
        "gla_w_gk2": nrm((N_A, GLA_GATE_RANK, GLA_HEADS * GLA_DK), GLA_GATE_RANK ** -0.5),
        "gla_b_gk2": nrm((N_A, GLA_HEADS * GLA_DK), 0.1),
        "gla_g_onorm": gain((N_A, GLA_DV)),
        "gla_w_out": nrm((N_A, GLA_HEADS * GLA_DV, D), (GLA_HEADS * GLA_DV) ** -0.5),
        "kv_g_norm": gain((D,)),
        "kv_w_in": nrm((D, KV_IN), D ** -0.5),
        "kv_g_knorm": gain((FOX_HD,)),
        "kv_b_f": 1.0 + 4.0 * jax.random.uniform(next(ks), (H,), jnp.float32),
        "fox_w_in": nrm((N_B, D, 2 * D), D ** -0.5),
        "fox_g_qnorm": gain((N_B, FOX_HD)),
        "fox_w_out": nrm((N_B, D, D), D ** -0.5),
        "ffn_w_up": nrm((DEPTH, D, 2 * D_FF), D ** -0.5),
        "ffn_w_dw": nrm((DEPTH, CONV_W, 2 * D_FF), CONV_W ** -0.5),
        "ffn_b_dw": nrm((DEPTH, 2 * D_FF), 0.02),
        "ffn_w_down": nrm((DEPTH, D_FF, D), D_FF ** -0.5),
        "ple_w_gate": nrm((DEPTH, D, D), D ** -0.5),
        "ple_b_gate": nrm((DEPTH, D), 0.02),
        "ple_w_proj": nrm((DEPTH, PLE_DIM, D), PLE_DIM ** -0.5),
        "g_final": gain((D,)),
    }


def reference(x, p, g_mix, g_ffn, g_ple, gla_w_in, gla_w_gk2, gla_b_gk2, gla_g_onorm,
              gla_w_out, kv_g_norm, kv_w_in, kv_g_knorm, kv_b_f, fox_w_in, fox_g_qnorm,
              fox_w_out, ffn_w_up, ffn_w_dw, ffn_b_dw, ffn_w_down, ple_w_gate, ple_b_gate,
              ple_w_proj, g_final):
    k_sh = v_sh = c_sh = None
    for i in range(DEPTH):
        if i == N_A:
            k_sh, v_sh, c_sh = _shared_kv(x, kv_g_norm, kv_w_in, kv_g_knorm, kv_b_f)
        h = _rmsnorm(x, g_mix[i])
        if i < N_A:
            x = x + _gla_mixer(h, gla_w_in[i], gla_w_gk2[i], gla_b_gk2[i],
                               gla_g_onorm[i], gla_w_out[i])
        else:
            j = i - N_A
            x = x + _fox_mixer(h, fox_w_in[j], fox_g_qnorm[j], fox_w_out[j], k_sh, v_sh, c_sh)
        x = x + _conv_ffn(_rmsnorm(x, g_ffn[i]), ffn_w_up[i], ffn_w_dw[i], ffn_b_dw[i],
                          ffn_w_down[i])
        x = x + _ple(x, p[i], g_ple[i], ple_w_gate[i], ple_b_gate[i], ple_w_proj[i])
    return _rmsnorm(x, g_final)
```

```python
from contextlib import ExitStack
import numpy as np
import ml_dtypes
import concourse.bass as bass
import concourse.mybir as mybir
from concourse.bass_utils import run_bass_kernel_spmd

F32 = mybir.dt.float32
BF16 = mybir.dt.bfloat16
AF = mybir.ActivationFunctionType
ALU = mybir.AluOpType
AX = mybir.AxisListType

D = 1024
T = 2048
NT = 16
DFF = 2816
NFC = 22
EPS = 1e-6
NCORES = 8


class Buf:
    __slots__ = ("name", "w", "r")

    def __init__(self, name=""):
        self.name = name
        self.w = None
        self.r = {}


class Sched:
    ENG = ("pe", "act", "dve", "pool", "sp")

    def __init__(self, nc, es):
        self.nc = nc
        self.es = es
        self.streams = {k: [] for k in self.ENG}
        self.sems = {}
        self.cnt = {}
        self.waited = {k: {} for k in self.ENG}
        for k in self.ENG:
            self._mksem(k)

    def _mksem(self, key):
        self.sems[key] = self.es.enter_context(self.nc.semaphore("s_" + key))
        self.cnt[key] = 0

    def _need(self, eng, ev, same_ok):
        if ev is None:
            return
        key, val = ev
        if same_ok and key == eng:
            return
        if self.waited[eng].get(key, 0) >= val:
            return
        self.waited[eng][key] = val
        self.streams[eng].append(("w", key, val))

    def _deps(self, eng, reads, writes, is_dma):
        for b in reads:
            self._need(eng, b.w, False)
        for b in writes:
            self._need(eng, b.w, not is_dma)
            for k, v in b.r.items():
                self._need(eng, (k, v), not is_dma)

    def _mark(self, ev, reads, writes):
        for b in writes:
            b.w = ev
            b.r = {}
        for b in reads:
            if b.r.get(ev[0], 0) < ev[1]:
                b.r[ev[0]] = ev[1]

    def op(self, eng, fn, reads=(), writes=()):
        self._deps(eng, reads, writes, False)
        self.cnt[eng] += 1
        ev = (eng, self.cnt[eng])
        self.streams[eng].append(("o", fn, eng, 1))
        self._mark(ev, reads, writes)
        return ev

    def dma(self, eng, semkey, out, in_, reads=(), writes=(), after=()):
        if semkey not in self.sems:
            self._mksem(semkey)
        for ev in after:
            self._need(eng, ev, False)
        self._deps(eng, reads, writes, True)
        self.cnt[semkey] += 16
        ev = (semkey, self.cnt[semkey])
        self.streams[eng].append(("o", lambda e: e.dma_start(out=out, in_=in_), semkey, 16))
        self._mark(ev, reads, writes)
        return ev

    def barrier(self):
        evs = [(k, self.cnt[k]) for k in ("pe", "act", "dve", "pool") if self.cnt[k] > 0]
        for e in ("pe", "act", "dve", "pool"):
            for ev in evs:
                if ev[0] != e:
                    self._need(e, ev, False)
        return evs

    def emit(self):
        nc = self.nc
        with nc.Block() as block:
            def mk(key):
                def body(e):
                    for it in self.streams[key]:
                        if it[0] == "w":
                            e.wait_ge(self.sems[it[1]], it[2])
                        else:
                            it[1](e).then_inc(self.sems[it[2]], it[3])
                return body
            block.tensor(mk("pe"))
            block.scalar(mk("act"))
            block.vector(mk("dve"))
            block.gpsimd(mk("pool"))
            block.sync(mk("sp"))


WNAMES = [("gla_w_in", 1024, 3088), ("gla_w_out", 1024, 1024), ("kv_w_in", 1024, 2064),
          ("fox_w_in", 1024, 2048), ("fox_w_out", 1024, 1024),
          ("ffn_w_up0", 1024, 5632), ("ffn_w_up1", 1024, 5632),
          ("ffn_w_down0", 2816, 1024), ("ffn_w_down1", 2816, 1024),
          ("ple_w_gate0", 1024, 1024), ("ple_w_gate1", 1024, 1024),
          ("ple_w_proj0", 256, 1024), ("ple_w_proj1", 256, 1024),
          ("gk2aug", 17, 512), ("bgate", 2, 1024)]

CP_G = {"g_mix0": 0, "g_ffn0": 8, "g_ple0": 16, "kv_g": 24, "g_mix1": 32, "g_ffn1": 40, "g_ple1": 48,
        "gon": 56}
CP_WDW = 64
CP_BDW = 64 + 264
NCP = 64 + 264 + 88
RW_GFIN = 0
RW_GQ = 0
RW_GK = 64
RW_BF = 128
NRW = 144


def _host_consts():
    s = np.arange(128)[:, None]
    c = np.arange(128)[None, :]
    le = (s <= c).astype(np.float32)
    gt = (s > c).astype(np.float32)
    cf = np.concatenate([le, -le / 16.0, -gt / 16.0, -le, -np.ones((128, 128), np.float32)], axis=1)
    cb = np.concatenate([np.eye(128, dtype=np.float32), -30000.0 * gt], axis=1).astype(ml_dtypes.bfloat16)
    return np.ascontiguousarray(cf, np.float32), np.ascontiguousarray(cb)


def _fm(v):
    v = np.asarray(v, np.float32)
    return np.ascontiguousarray(v.reshape(-1, 128).T)


def _prep_shared(inp):
    sh = {}
    sh["gla_w_in"] = inp["gla_w_in"][0]
    sh["gla_w_out"] = inp["gla_w_out"][0]
    sh["kv_w_in"] = inp["kv_w_in"]
    sh["fox_w_in"] = inp["fox_w_in"][0]
    sh["fox_w_out"] = inp["fox_w_out"][0]
    for l in range(2):
        sh[f"ffn_w_up{l}"] = inp["ffn_w_up"][l]
        sh[f"ffn_w_down{l}"] = inp["ffn_w_down"][l]
        sh[f"ple_w_gate{l}"] = inp["ple_w_gate"][l]
        sh[f"ple_w_proj{l}"] = inp["ple_w_proj"][l]
    sh["gk2aug"] = np.concatenate([inp["gla_w_gk2"][0], inp["gla_b_gk2"][0][None, :]], axis=0)
    sh["bgate"] = inp["ple_b_gate"]
    cp = np.zeros((128, NCP), np.float32)
    cp[:, 0:8] = _fm(inp["g_mix"][0]); cp[:, 8:16] = _fm(inp["g_ffn"][0]); cp[:, 16:24] = _fm(inp["g_ple"][0])
    cp[:, 24:32] = _fm(inp["kv_g_norm"]); cp[:, 32:40] = _fm(inp["g_mix"][1])
    cp[:, 40:48] = _fm(inp["g_ffn"][1]); cp[:, 48:56] = _fm(inp["g_ple"][1])
    cp[:, 56:64] = _fm(np.tile(inp["gla_g_onorm"][0], 4))
    for l in range(2):
        for tap in range(3):
            o = CP_WDW + (l * 3 + tap) * 44
            cp[:, o:o + 44] = _fm(inp["ffn_w_dw"][l, tap])
        o = CP_BDW + l * 44
        cp[:, o:o + 44] = _fm(inp["ffn_b_dw"][l])
    sh["cpp"] = cp
    rw = np.zeros((1, NRW), np.float32)
    rw[0, RW_GQ:RW_GQ + 64] = inp["fox_g_qnorm"][0]
    rw[0, RW_GK:RW_GK + 64] = inp["kv_g_knorm"]
    rw[0, RW_BF:RW_BF + 16] = inp["kv_b_f"]
    sh["roww"] = rw
    sh["gfin"] = np.asarray(inp["g_final"], np.float32).reshape(1, 1024)
    sh["gonrow"] = np.tile(np.asarray(inp["gla_g_onorm"][0], np.float32), 4).reshape(1, 1024)
    cf, cb = _host_consts()
    sh["cf32"] = cf
    sh["cbf"] = cb
    return {k: np.ascontiguousarray(np.asarray(v)) for k, v in sh.items()}


class KB:
    def __init__(self, nc, es, nseq):
        self.nc = nc
        self.es = es
        self.nseq = nseq
        self.S = Sched(nc, es)
        self.uid = 0
        self.after = []
        self.pst = [es.enter_context(nc.psum_tensor(f"ps{i}", [128, 1024], F32)) for i in range(4)]
        self.psb = [Buf(f"psb{i}") for i in range(8)]
        self.pspos = 0
        self.psn = 8

    def sb(self, scope, name, shape, dt):
        self.uid += 1
        return scope.enter_context(self.nc.sbuf_tensor(f"{name}_{self.uid}", shape, dt))

    def ps(self, n=1):
        if n == 2 and self.pspos % 2:
            self.pspos += 1
        b = self.pspos % self.psn
        if b + n > self.psn:
            self.pspos += self.psn - b
            b = 0
        self.pspos += n
        t = self.pst[b // 2]
        off = (b % 2) * 512
        return t[:, off:off + 512 * n], self.psb[b:b + n]

    def mm(self, out, lhsT, rhs, r, w, start=True, stop=True):
        self.S.op("pe", lambda e: e.matmul(out, lhsT, rhs, start=start, stop=stop), r, w)

    def tr(self, out, in_, r, w):
        kp = in_.shape[0]
        idt = self.ident[0:kp, 0:kp]
        self.S.op("pe", lambda e: e.transpose(out, in_, idt), list(r) + [self.cbb], w)

    def act(self, out, in_, func, r, w, bias=None, scale=None, accum=None):
        kw = {}
        if bias is not None:
            kw["bias"] = bias
        if scale is not None:
            kw["scale"] = scale
        if accum is not None:
            kw["accum_out"] = accum
        self.S.op("act", lambda e: e.activation(out, in_, func, **kw), r, w)

    def amul(self, out, in_, mul, r, w):
        self.S.op("act", lambda e: e.mul(out, in_, mul), r, w)

    def tt(self, eng, out, in0, in1, op, r, w):
        self.S.op(eng, lambda e: e.tensor_tensor(out, in0, in1, op), r, w)

    def stt(self, eng, out, in0, scalar, in1, op0, op1, r, w):
        self.S.op(eng, lambda e: e.scalar_tensor_tensor(out, in0, scalar, in1, op0, op1), r, w)

    def ts(self, eng, out, in0, s1, s2, op0, op1, r, w):
        if s2 is None:
            self.S.op(eng, lambda e: e.tensor_scalar(out, in0, s1, None, op0), r, w)
        else:
            self.S.op(eng, lambda e: e.tensor_scalar(out, in0, s1, s2, op0, op1), r, w)

    def cp(self, eng, out, in_, r, w):
        if eng == "act":
            self.S.op("act", lambda e: e.copy(out, in_), r, w)
        else:
            self.S.op(eng, lambda e: e.tensor_copy(out, in_), r, w)

    def memset(self, eng, ap, val, w):
        self.S.op(eng, lambda e: e.memset(ap, val), (), w)

    def recip(self, out, in_, r, w):
        self.S.op("dve", lambda e: e.reciprocal(out, in_), r, w)

    def barrier(self):
        self.after = self.S.barrier()
        self.wslots = self.wslots[:2]
        self.wslotb = self.wslotb[:2]
        self.wpos = 0

    def wtile(self, wname, kc0, nkc, c0, ncols):
        i = self.wpos % len(self.wslots)
        self.wpos += 1
        slot, b = self.wslots[i], self.wslotb[i]
        src = self.wd[wname].rearrange("(c p) n -> p c n", p=128)[:, kc0:kc0 + nkc, c0:c0 + ncols]
        self.S.dma("sp", f"wr{i}", slot[:, 0:nkc, 0:ncols], src, reads=[self.wdb[wname]], writes=[b],
                   after=(self.after if i >= 2 else ()))
        return slot, b

    def extra_slots(self, scope, n):
        self.wslots = self.wslots[:2] + [self.sb(scope, f"wsx{i}", [128, 8, 512], BF16) for i in range(n)]
        self.wslotb = self.wslotb[:2] + [Buf(f"wsx{i}") for i in range(n)]
        self.wpos = 0

    def setup(self, dram):
        nc, es, S = self.nc, self.es, self.S
        self.dram = dram
        self.wd, self.wdb = {}, {}
        for name, K, N in WNAMES:
            self.wd[name] = nc.dram_tensor("wb_" + name, [K, N], BF16, kind="Internal").ap()
            self.wdb[name] = Buf("wd_" + name)
        order = ["gk2aug", "bgate", "gla_w_in", "gla_w_out", "ffn_w_up0", "ffn_w_down0", "ple_w_gate0",
                 "ple_w_proj0", "kv_w_in", "fox_w_in", "fox_w_out", "ffn_w_up1", "ffn_w_down1",
                 "ple_w_gate1", "ple_w_proj1"]
        dims = {n: (K, N) for n, K, N in WNAMES}
        self.wdb["gla_w_in_qk"] = Buf("wd_gla_w_in_qk")
        for (c0, c1) in ((0, 1024), (3072, 3088)):
            for r0 in range(0, 1024, 256):
                S.dma("pool", "cv_gla_qk", self.wd["gla_w_in"][r0:r0 + 256, c0:c1],
                      dram["gla_w_in"][r0:r0 + 256, c0:c1], writes=[self.wdb["gla_w_in_qk"]])
        for name in order:
            K, N = dims[name]
            if name == "gla_w_in":
                for r0 in range(0, 1024, 128):
                    S.dma("pool", "cv_" + name, self.wd[name][r0:r0 + 128, 1024:3072],
                          dram[name][r0:r0 + 128, 1024:3072], writes=[self.wdb[name]])
                continue
            r0 = 0
            while r0 < K:
                r1 = min(K, r0 + 128)
                S.dma("pool", "cv_" + name, self.wd[name][r0:r1, :], dram[name][r0:r1, :],
                      writes=[self.wdb[name]])
                r0 = r1
        self.cpp = self.sb(es, "cpp", [128, NCP], F32)
        self.roww = self.sb(es, "roww", [128, NRW], F32)
        self.cf32 = self.sb(es, "cf32", [128, 640], F32)
        self.cbf = self.sb(es, "cbf", [128, 256], BF16)
        self.bgate = self.sb(es, "bgate", [1, 2, 1024], BF16)
        self.ones = self.sb(es, "ones", [1, 128], BF16)
        self.cbb = Buf("consts")
        S.dma("sp", "cst", self.cpp[:], dram["cpp"], writes=[self.cbb])
        S.dma("sp", "cst", self.roww[:], dram["roww"].partition_broadcast(128), writes=[self.cbb])
        S.dma("sp", "cst", self.cf32[:], dram["cf32"], writes=[self.cbb])
        S.dma("sp", "cst", self.cbf[:], dram["cbf"], writes=[self.cbb])
        S.dma("sp", "cst", self.bgate[:], self.wd["bgate"].rearrange("(o l) n -> o l n", o=1),
              reads=[self.wdb["bgate"]], writes=[self.cbb])
        self.memset("dve", self.ones[:], 1.0, [self.cbb])
        self.ident = self.cbf[:, 0:128]
        self.maskneg = self.cbf[:, 128:256]
        self.mask01 = self.cf32[:, 0:128]
        self.mcum = self.cf32[:, 128:256]
        self.mrev = self.cf32[:, 256:384]
        self.trineg = self.cf32[:, 384:512]
        self.onesneg = self.cf32[:, 512:640]
        self.X = self.sb(es, "X", [128, NT, D], F32)
        self.Xb = [Buf(f"X{i}") for i in range(NT)]
        self.wslots = [self.sb(es, f"wslot{i}", [128, 8, 512], BF16) for i in range(2)]
        self.wslotb = [Buf(f"wslot{i}") for i in range(2)]
        self.wpos = 0
        self.pbF = self.sb(es, "pbF", [128, 4, 256], BF16)
        self.pbFb = Buf("pbF")
        self.nss2 = [self.sb(es, f"nss{i}", [128, 16], F32) for i in range(2)]
        self.nss2b = [Buf(f"nss{i}") for i in range(2)]
        self.sidx = 0
        self.nss, self.nssb = self.nss2[0], self.nss2b[0]
        self.nsq = self.sb(es, "nsq", [128, 16], F32); self.nsqb = Buf("nsq")
        self.nrs = self.sb(es, "nrs", [128, 16], F32); self.nrsb = Buf("nrs")
        self.xn2 = [self.sb(es, f"xn{i}", [128, 1024], BF16) for i in range(2)]
        self.xn2b = [Buf(f"xn{i}") for i in range(2)]
        self.junk, self.junkb = self.xn2[0], self.xn2b[0]

    def load_x(self, seq):
        self.cur_seq = seq
        self.stats_begin()
        if getattr(self, "preloaded", False):
            self.preloaded = False
        else:
            for i in range(NT):
                self.S.dma("sp", f"xl{i}", self.X[:, i, :], self.dram["x"][seq, i * 128:(i + 1) * 128, :],
                           writes=[self.Xb[i]])
        self.fresh_x = True

    def stats_begin(self):
        self.rd, self.rdb = self.nss, self.nssb
        self.sidx ^= 1
        self.nss, self.nssb = self.nss2[self.sidx], self.nss2b[self.sidx]
        self.memset("dve", self.nss[:], 0.0, [self.nssb])

    def x_final(self, ti):
        j = self.xn2[ti % 2]
        self.act(j[:], self.X[:, ti, :], AF.Square, [self.Xb[ti], self.nssb], [self.xn2b[ti % 2], self.nssb],
                 accum=self.nss[:, ti:ti + 1])

    def phase_rstd(self, t0=0, nt=NT, src=None):
        ss, ssb = src if src is not None else (self.nss, self.nssb)
        c = slice(t0, t0 + nt)
        self.act(self.nsq[:, c], ss[:, c], AF.Sqrt, [ssb], [self.nsqb], bias=EPS, scale=1.0 / D)
        self.recip(self.nrs[:, c], self.nsq[:, c], [self.nsqb], [self.nrsb])

    def norm_gen(self, t0, nt, gcol, HT, HTb):
        rs, rsb = self.nrs, self.nrsb
        htb = HTb if isinstance(HTb, list) else [HTb] * nt
        gbc = self.cpp[:, gcol:gcol + 8].unsqueeze(2).broadcast_to([128, 8, 128])
        for i in range(nt):
            xn, xnb = self.xn2[i % 2], self.xn2b[i % 2]
            self.amul(xn[:], self.X[:, t0 + i, :], rs[:, t0 + i:t0 + i + 1], [self.Xb[t0 + i], rsb], [xnb])
            pt, pb = self.ps(1)
            ptb = pt.bitcast(BF16)
            for kc in range(8):
                self.tr(ptb[:, kc * 128:(kc + 1) * 128], xn[:, kc * 128:(kc + 1) * 128], [xnb], pb)
            self.tt("dve", HT[:, :, i * 128:(i + 1) * 128], ptb.rearrange("p (c t) -> p c t", t=128), gbc,
                    ALU.mult, pb + [self.cbb], [htb[i]])
            yield i

    def norm_group(self, t0, nt, gcol, HT, HTb):
        for _ in self.norm_gen(t0, nt, gcol, HT, HTb):
            pass

    def gla(self):
        S = self.S
        fresh = getattr(self, "fresh_x", False)
        self.fresh_x = False
        if fresh:
            for i in range(8):
                self.x_final(i)
        self.stats_begin()
        NQ = 1040
        with ExitStack() as sc:
            self.extra_slots(sc, 2)
            win = self.sb(sc, "win", [128, 8, NQ], BF16); winb = Buf("win")
            wsrc = self.wd["gla_w_in"].rearrange("(c p) n -> p c n", p=128)
            S.dma("sp", "winl", win[:, :, 0:1024], wsrc[:, :, 0:1024], reads=[self.wdb["gla_w_in_qk"]],
                  writes=[winb], after=self.after)
            S.dma("sp", "winl", win[:, :, 1024:1040], wsrc[:, :, 3072:3088], reads=[self.wdb["gla_w_in_qk"]],
                  writes=[winb], after=self.after)
            gk2 = self.sb(sc, "gk2", [32, 512], BF16); gk2b = Buf("gk2")
            S.dma("sp", "gk2l", gk2[0:17, :], self.wd["gk2aug"], reads=[self.wdb["gk2aug"]], writes=[gk2b],
                  after=self.after)
            HT = self.sb(sc, "HT", [128, 8, 512], BF16); HTb = [Buf(f"HT{i}") for i in range(4)]
            AT, ATb = HT, HTb
            gr = self.sb(sc, "gr", [32, 512], BF16); grb = Buf("gr")
            sp = [self.sb(sc, f"sp{i}", [128, 512], F32) for i in range(2)]; spb = [Buf(), Buf()]
            eq = self.sb(sc, "eq", [128, 4, 512], F32); eqb = Buf("eq")
            ek = self.sb(sc, "ek", [128, 4, 512], F32); ekb = Buf("ek")
            een = [self.sb(sc, f"een{i}", [128, 512], F32) for i in range(2)]; eenb = [Buf(), Buf()]
            qin = self.sb(sc, "qin", [128, 4, 512], BF16); qinb = Buf("qin")
            kin = self.sb(sc, "kin", [128, 4, 512], BF16); kinb = Buf("kin")
            kend = [self.sb(sc, f"kend{i}", [128, 512], BF16) for i in range(4)]; kendb = [Buf() for _ in range(4)]
            vg = self.sb(sc, "vg", [128, 4, 1024], BF16); vgb = [Buf() for _ in range(4)]
            sgg = self.sb(sc, "sgg", [128, 4, 1024], BF16); sggb = [Buf() for _ in range(4)]
            attm = self.sb(sc, "attm", [128, 4, 128], BF16); attmb = Buf("attm")
            St = self.sb(sc, "St", [128, 4, 256], F32); Stb = Buf("St")
            Sbf = self.sb(sc, "Sbf", [128, 4, 256], BF16); Sbfb = Buf("Sbf")
            osq = self.sb(sc, "osq", [128, 4], F32); osqb = Buf("osq")
            ors = self.sb(sc, "ors", [128, 4], F32); orsb = Buf("ors")
            ojk = self.sb(sc, "ojk", [128, 256], BF16); ojkb = Buf("ojk")
            aa = [self.sb(sc, f"aa{i}", [128, 1024], BF16) for i in range(2)]; aab = [Buf(), Buf()]
            gonrow = self.sb(sc, "gonrow", [128, 1024], F32); gonrowb = Buf("gonrow")
            S.dma("sp", "gonl", gonrow[:], self.dram["gonrow"].partition_broadcast(128), writes=[gonrowb],
                  after=self.after)
            sgt, sgtb = een, eenb

            self.memset("dve", gr[:], 1.0, [grb])
            gonbc = self.cpp[:, CP_G["gon"]:CP_G["gon"] + 8].unsqueeze(2).broadcast_to([128, 8, 128])
            m01 = self.mask01.unsqueeze(1).broadcast_to([128, 4, 128])
            pend_final = []
            for g in range(4):
                self.phase_rstd(4 * g, 4, src=(self.rd, self.rdb))
                lazy = []
                if fresh and g + 2 < 4:
                    lazy = list(range(4 * (g + 2), 4 * (g + 3)))
                if False:
                    sv = (self.nss, self.nssb)
                    self.nss, self.nssb = self.rd, self.rdb
                    for i in range(4 * (g + 2), 4 * (g + 3)):
                        self.x_final(i)
                    self.nss, self.nssb = sv
                ngen = self.norm_gen(4 * g, 4, CP_G["g_mix0"], HT, HTb)
                next(ngen, None)
                for which in range(2):
                    for cg in range(2):
                        wt, wtb = self.wtile("gla_w_in", 0, 8, 1024 + which * 1024 + cg * 512, 512)
                        for i in range(4):
                            next(ngen, None)
                            tok = slice(i * 128, (i + 1) * 128)
                            pv, pvb = self.ps(1)
                            for kc in range(8):
                                self.mm(pv, HT[:, kc, tok], wt[:, kc, :], [wtb, HTb[i]], pvb,
                                        start=(kc == 0), stop=(kc == 7))
                            if which == 0:
                                self.cp("act", vg[:, i, cg * 512:(cg + 1) * 512], pv, pvb, [vgb[i]])
                            else:
                                k_ = (cg * 4 + i) % 2
                                self.act(sgt[k_][:], pv, AF.Silu, pvb, [sgtb[k_]])
                                self.tt("dve", sgg[:, i, cg * 512:(cg + 1) * 512], sgt[k_][:],
                                        gonrow[:, cg * 512:(cg + 1) * 512], ALU.mult, [sgtb[k_], gonrowb], [sggb[i]])
                        if which == 0 and cg == 0:
                            for ti_ in pend_final:
                                self.x_final(ti_)
                            pend_final = []
                            if lazy:
                                sv = (self.nss, self.nssb)
                                self.nss, self.nssb = self.rd, self.rdb
                                for ti_ in lazy:
                                    self.x_final(ti_)
                                self.nss, self.nssb = sv
                pg, pgb = self.ps(1)
                for kc in range(8):
                    self.mm(pg[0:16, :], win[:, kc, 1024:1040], HT[:, kc, :], [winb] + HTb, pgb,
                            start=(kc == 0), stop=(kc == 7))
                self.cp("dve", gr[0:16, :], pg[0:16, :], pgb, [grb])
                for i in range(4):
                    tok = slice(i * 128, (i + 1) * 128)
                    spt, sptb = sp[i % 2], spb[i % 2]
                    pz, pzb = self.ps(1)
                    self.mm(pz, gr[0:17, tok], gk2[0:17, :], [grb, gk2b], pzb)
                    self.act(spt[:], pz, AF.Exp, pzb, [sptb], scale=-1.0)
                    self.act(spt[:], spt[:], AF.Ln, [sptb], [sptb], bias=1.0)
                    pbt, pbtb = self.ps(1)
                    for h in range(4):
                        self.mm(pbt[:, h * 128:(h + 1) * 128], spt[:, h * 128:(h + 1) * 128], self.mcum,
                                [sptb, self.cbb], pbtb)
                    pb3 = pbt.rearrange("p (h c) -> p h c", c=128)
                    self.act(eq[:, :, tok], pb3, AF.Exp, pbtb, [eqb])
                    self.act(ek[:, :, tok], pb3, AF.Exp, pbtb, [ekb], scale=-1.0)
                    prv, prvb = self.ps(1)
                    self.mm(prv, self.mrev, spt[:], [sptb, self.cbb], prvb)
                    self.act(een[i % 2][:], prv, AF.Exp, prvb, [eenb[i % 2]])
                    pk2, pk2b = self.ps(1)
                    for kc in range(8):
                        self.mm(pk2, HT[:, kc, tok], win[:, kc, 512:1024], [winb, HTb[i]], pk2b,
                                start=(kc == 0), stop=(kc == 7))
                    self.tt("dve", kend[i][:], pk2, een[i % 2][:], ALU.mult, pk2b + [eenb[i % 2]], [kendb[i]])
                for h in range(4):
                    pq, pqb = self.ps(1)
                    for kc in range(8):
                        self.mm(pq, win[:, kc, h * 128:(h + 1) * 128], HT[:, kc, :], [winb] + HTb, pqb,
                                start=(kc == 0), stop=(kc == 7))
                    self.stt("dve", qin[:, h, :], pq, 128.0 ** -0.5, eq[:, h, :], ALU.mult, ALU.mult,
                             pqb + [eqb], [qinb])
                    pk, pkb = self.ps(1)
                    for kc in range(8):
                        self.mm(pk, win[:, kc, 512 + h * 128:512 + (h + 1) * 128], HT[:, kc, :], [winb] + HTb,
                                pkb, start=(kc == 0), stop=(kc == 7))
                    self.tt("dve", kin[:, h, :], pk, ek[:, h, :], ALU.mult, pkb + [ekb], [kinb])
                pendE = None

                def stageE(pe):
                    a_, ab_, tok_, ie_ = pe
                    pt, ptb_ = self.ps(1)
                    ptb = pt.bitcast(BF16)
                    for kc in range(8):
                        self.tr(ptb[:, kc * 128:(kc + 1) * 128], a_[:, kc * 128:(kc + 1) * 128], [ab_], ptb_)
                    self.cp("act", AT[:, :, tok_], ptb.rearrange("p (c t) -> p c t", t=128), ptb_, [ATb[ie_]])

                for i in range(4):
                    ti = 4 * g + i
                    tok = slice(i * 128, (i + 1) * 128)
                    pa, pab = self.ps(1)
                    for h in range(4):
                        self.mm(pa[:, h * 128:(h + 1) * 128], kin[:, h, tok], qin[:, h, tok], [kinb, qinb], pab)
                    self.tt("dve", attm[:], pa.rearrange("p (h c) -> p h c", c=128), m01, ALU.mult,
                            pab + [self.cbb], [attmb])
                    if ti < NT - 1:
                        pS, pSb = self.ps(2)
                        for h in range(4):
                            hs = slice(h * 256, (h + 1) * 256)
                            self.mm(pS[:, hs], kend[i][:, h * 128:(h + 1) * 128], vg[:, i, hs],
                                    [kendb[i], vgb[i]], pSb)
                    pO, pOb = self.ps(2)
                    for h in range(4):
                        hs = slice(h * 256, (h + 1) * 256)
                        self.mm(pO[:, hs], attm[:, h, :], vg[:, i, hs], [attmb, vgb[i]], pOb, start=True,
                                stop=(ti == 0))
                        if ti > 0:
                            self.mm(pO[:, hs], qin[:, h, tok], Sbf[:, h, :], [qinb, Sbfb], pOb,
                                    start=False, stop=True)
                    if ti < NT - 1:
                        for h in range(4):
                            hs = slice(h * 256, (h + 1) * 256)
                            if ti == 0:
                                self.cp("dve", St[:, h, :], pS[:, hs], pSb, [Stb])
                            else:
                                dcol = eq[:, h, i * 128 + 127:i * 128 + 128]
                                self.stt("dve", St[:, h, :], St[:, h, :], dcol, pS[:, hs], ALU.mult, ALU.add,
                                         pSb + [Stb, eqb], [Stb])
                        self.cp("act", Sbf[:], St[:], [Stb], [Sbfb])
                    self.memset("dve", osq[:], 0.0, [osqb])
                    for h in range(4):
                        hs = slice(h * 256, (h + 1) * 256)
                        self.act(ojk[:], pO[:, hs], AF.Square, pOb + [osqb], [ojkb, osqb], accum=osq[:, h:h + 1])
                    self.act(osq[:], osq[:], AF.Sqrt, [osqb], [osqb], bias=EPS, scale=1.0 / 256)
                    self.recip(ors[:], osq[:], [osqb], [orsb])
                    a, ab = aa[i % 2], aab[i % 2]
                    for h in range(4):
                        hs = slice(h * 256, (h + 1) * 256)
                        self.stt("dve", a[:, hs], pO[:, hs], ors[:, h:h + 1], sgg[:, i, hs], ALU.mult, ALU.mult,
                                 pOb + [orsb, sggb[i]], [ab])
                    if pendE is not None:
                        stageE(pendE)
                    pendE = (a, ab, tok, i)
                stageE(pendE)
                for cg in range(2):
                    wt, wtb = self.wtile("gla_w_out", 0, 8, cg * 512, 512)
                    for i in range(4):
                        ti = 4 * g + i
                        px, pxb = self.ps(1)
                        for kc in range(8):
                            self.mm(px, AT[:, kc, i * 128:(i + 1) * 128], wt[:, kc, :], [ATb[i], wtb], pxb,
                                    start=(kc == 0), stop=(kc == 7))
                        xs = self.X[:, ti, cg * 512:(cg + 1) * 512]
                        self.tt("dve", xs, xs, px, ALU.add, pxb + [self.Xb[ti]], [self.Xb[ti]])
                        if cg == 1:
                            pend_final.append(ti)
            for ti in pend_final:
                self.x_final(ti)
        self.barrier()

    def ffn(self, l):
        S = self.S
        G = 1024
        wn_up, wn_dn = f"ffn_w_up{l}", f"ffn_w_down{l}"
        gcol = CP_G[f"g_ffn{l}"]
        self.stats_begin()
        with ExitStack() as sc:
            self.extra_slots(sc, 2)
            HT = self.sb(sc, "HTf", [128, 8, G], BF16); HTb = Buf("HTf")
            AT = self.sb(sc, "ATf", [128, NFC, G], BF16); ATb = Buf("ATf")
            uur = [self.sb(sc, f"uu{i}", [128, G + 2], F32) for i in range(3)]; uurb = [Buf() for _ in range(3)]
            uurh = [Buf() for _ in range(3)]
            ccr = [self.sb(sc, f"cc{i}", [128, G], F32) for i in range(4)]; ccrb = [Buf() for _ in range(4)]
            rpos = 0

            def finish_pair(jp, resp):
                (cA, cAb), (cB, cBb) = resp
                self.act(cA[:], cA[:], AF.Silu, [cAb], [cAb])
                self.tt("dve", AT[:, jp, :], cA[:], cB[:], ALU.mult, [cAb, cBb], [ATb])
            hal = self.sb(sc, "hal", [128, 44, 2], F32); halb = Buf("hal")
            self.memset("pool", hal[:], 0.0, [halb])
            psrc = self.dram["p"][l, self.cur_seq, 0:512, :].rearrange("(i t) c -> t i c", t=128)
            S.dma("pool", "pldF", self.pbF[:], psrc, writes=[self.pbFb])
            ngen = None
            for g in range(T // G):
                if ngen is None:
                    self.phase_rstd(8 * g, 8, src=(self.rd, self.rdb))
                    self.norm_group(8 * g, 8, gcol, HT, HTb)
                else:
                    for _ in ngen:
                        pass
                ngen = None
                pend = None
                for j in range(NFC):
                    slot_i = self.wpos % len(self.wslots)
                    self.wpos += 1
                    wt, wtb = self.wslots[slot_i], self.wslotb[slot_i]
                    wsrc = self.wd[wn_up].rearrange("(c p) n -> p c n", p=128)
                    aft = self.after if slot_i >= 2 else ()
                    S.dma("sp", f"wr{slot_i}", wt[:, :, 0:128], wsrc[:, :, j * 128:(j + 1) * 128],
                          reads=[self.wdb[wn_up]], writes=[wtb], after=aft)
                    S.dma("sp", f"wr{slot_i}", wt[:, :, 128:256],
                          wsrc[:, :, DFF + j * 128:DFF + (j + 1) * 128], reads=[self.wdb[wn_up]], writes=[wtb],
                          after=aft)
                    res = []
                    for half, jj in enumerate([j, NFC + j]):
                        uu, uub, cc, ccb = uur[rpos % 3], uurb[rpos % 3], ccr[rpos % 4], ccrb[rpos % 4]
                        uuh = uurh[rpos % 3]
                        rpos += 1
                        self.cp("pool", uu[:, 0:2], hal[:, jj, :], [halb], [uuh])
                        for tg in range(G // 512):
                            pu, pub = self.ps(1)
                            for kc in range(8):
                                self.mm(pu, wt[:, kc, half * 128:(half + 1) * 128],
                                        HT[:, kc, tg * 512:(tg + 1) * 512], [wtb, HTb], pub,
                                        start=(kc == 0), stop=(kc == 7))
                            self.cp("act", uu[:, 2 + tg * 512:2 + (tg + 1) * 512], pu, pub, [uub])
                        self.cp("pool", hal[:, jj, :], uu[:, G:G + 2], [uub], [halb])
                        w0 = self.cpp[:, CP_WDW + (l * 3 + 0) * 44 + jj:CP_WDW + (l * 3 + 0) * 44 + jj + 1]
                        w1 = self.cpp[:, CP_WDW + (l * 3 + 1) * 44 + jj:CP_WDW + (l * 3 + 1) * 44 + jj + 1]
                        w2 = self.cpp[:, CP_WDW + (l * 3 + 2) * 44 + jj:CP_WDW + (l * 3 + 2) * 44 + jj + 1]
                        bb = self.cpp[:, CP_BDW + l * 44 + jj:CP_BDW + l * 44 + jj + 1]
                        self.act(cc[:], uu[:, 2:G + 2], AF.Identity, [uub, self.cbb], [ccb], bias=bb, scale=w2)
                        self.stt("dve", cc[:], uu[:, 1:G + 1], w1, cc[:], ALU.mult, ALU.add,
                                 [uub, uuh, ccb, self.cbb], [ccb])
                        self.stt("dve", cc[:], uu[:, 0:G], w0, cc[:], ALU.mult, ALU.add,
                                 [uub, uuh, ccb, self.cbb], [ccb])
                        res.append((cc, ccb))
                        if half == 0 and pend is not None:
                            finish_pair(*pend)
                    pend = (j, res)
                finish_pair(*pend)
                if g + 1 < T // G:
                    self.phase_rstd(8 * (g + 1), 8, src=(self.rd, self.rdb))
                    ngen = self.norm_gen(8 * (g + 1), 8, gcol, HT, HTb)
                for cg in range(2):
                    wts = []
                    for part, (k0, nk) in enumerate([(0, 8), (8, 8), (16, 6)]):
                        wts.append(self.wtile(wn_dn, k0, nk, cg * 512, 512) + (k0, nk))
                    for i in range(G // 128):
                        if ngen is not None and cg == 0:
                            next(ngen, None)
                        ti = 8 * g + i
                        px, pxb = self.ps(1)
                        for (wt, wtb, k0, nk) in wts:
                            for kk in range(nk):
                                j = k0 + kk
                                self.mm(px, AT[:, j, i * 128:(i + 1) * 128], wt[:, kk, :], [ATb, wtb], pxb,
                                        start=(j == 0), stop=(j == NFC - 1))
                        xs = self.X[:, ti, cg * 512:(cg + 1) * 512]
                        self.tt("dve", xs, xs, px, ALU.add, pxb + [self.Xb[ti]], [self.Xb[ti]])
                        if cg == 1:
                            self.x_final(ti)
        self.barrier()

    def ple(self, l, seq, fuse=None):
        S = self.S
        gcol = CP_G[f"g_ple{l}"]
        self.stats_begin()
        NG, NTG = 2, 8
        with ExitStack() as sc:
            self.extra_slots(sc, 2)
            if fuse is not None:
                NOB = 4
                ob = [self.sb(sc, f"fob{i}", [128, 1024], F32) for i in range(NOB)]
                obb = [Buf() for _ in range(NOB)]
                gfin = self.sb(sc, "gfinp", [128, 1024], F32); gfinb = Buf("gfinp")
                S.dma("sp", "cst", gfin[:], self.dram["gfin"].partition_broadcast(128), writes=[gfinb],
                      after=self.after)
            HT = self.sb(sc, "HTp", [128, 8, NTG * 128], BF16); HTb = [Buf(f"HTp{i}") for i in range(NTG)]
            pb16s = [self.sb(sc, f"pb16_{i}", [128, NTG, 256], BF16) for i in range(NG)]
            pb16bs = [Buf(f"pb16_{i}") for i in range(NG)]
            for g in range(NG):
                i0_ = 4 if g == 0 else 0
                src = self.dram["p"][l, seq, (g * NTG + i0_) * 128:(g + 1) * NTG * 128, :].rearrange(
                    "(i t) c -> t i c", t=128)
                S.dma("pool", f"pld{g}", pb16s[g][:, i0_:NTG, :], src, writes=[pb16bs[g]], after=self.after)
            pT = self.sb(sc, "pT", [128, 2, NTG * 128], BF16); pTb = [Buf(f"pT{i}") for i in range(NTG)]
            sgt = [self.sb(sc, f"sgp{i}", [128, 512], F32) for i in range(2)]; sgtb = [Buf(), Buf()]
            tmp = [self.sb(sc, f"tmpp{i}", [128, 512], F32) for i in range(2)]; tmpb = [Buf(), Buf()]
            for g in range(NG):
                pb16, pb16b = pb16s[g], pb16bs[g]
                self.phase_rstd(NTG * g, NTG, src=(self.rd, self.rdb))
                ngen = self.norm_gen(NTG * g, NTG, gcol, HT, HTb)
                next(ngen, None)

                def ptrans(i):
                    pt, ptb_ = self.ps(1)
                    ptb = pt.bitcast(BF16)
                    pin, pinb = (self.pbF, self.pbFb) if (g == 0 and i < 4) else (pb16, pb16b)
                    for c in range(2):
                        self.tr(ptb[:, c * 128:(c + 1) * 128], pin[:, i, c * 128:(c + 1) * 128], [pinb], ptb_)
                    self.cp("dve", pT[:, :, i * 128:(i + 1) * 128],
                            ptb[:, 0:256].rearrange("p (c t) -> p c t", t=128), ptb_, [pTb[i]])

                ptrans(0)
                for cg in range(2):
                    wg, wgb = self.wtile(f"ple_w_gate{l}", 0, 8, cg * 512, 512)
                    wp, wpb = self.wtile(f"ple_w_proj{l}", 0, 2, cg * 512, 512)
                    for i in range(NTG):
                        if cg == 0:
                            next(ngen, None)
                            if i + 1 < NTG:
                                ptrans(i + 1)
                        ti = NTG * g + i
                        tok = slice(i * 128, (i + 1) * 128)
                        k = (cg * NTG + i) % 2
                        pg, pgb = self.ps(1)
                        for kc in range(8):
                            self.mm(pg, HT[:, kc, tok], wg[:, kc, :], [HTb[i], wgb], pgb, start=(kc == 0), stop=False)
                        self.mm(pg, self.ones[0:1, :], self.bgate[0:1, l, cg * 512:(cg + 1) * 512], [self.cbb], pgb,
                                start=False, stop=True)
                        self.act(sgt[k][:], pg, AF.Sigmoid, pgb, [sgtb[k]])
                        pp, ppb = self.ps(1)
                        for c in range(2):
                            self.mm(pp, pT[:, c, tok], wp[:, c, :], [pTb[i], wpb], ppb, start=(c == 0), stop=(c == 1))
                        self.tt("dve", tmp[k][:], pp, sgt[k][:], ALU.mult, ppb + [sgtb[k]], [tmpb[k]])
                        xs = self.X[:, ti, cg * 512:(cg + 1) * 512]
                        self.tt("dve", xs, xs, tmp[k][:], ALU.add, [tmpb[k], self.Xb[ti]], [self.Xb[ti]])
                        if cg == 1:
                            self.x_final(ti)
                            if fuse is not None and ti % 4 == 3:
                                out_ap, nseq = fuse
                                self.phase_rstd(ti - 3, 4)
                                for tj in range(ti - 3, ti + 1):
                                    o, obf = ob[tj % NOB], obb[tj % NOB]
                                    self.stt("dve", o[:], self.X[:, tj, :], self.nrs[:, tj:tj + 1], gfin[:],
                                             ALU.mult, ALU.mult, [self.Xb[tj], self.nrsb, gfinb], [obf])
                                    S.dma("sp", f"ost{tj % NOB}", out_ap[seq, tj * 128:(tj + 1) * 128, :], o[:],
                                          reads=[obf], after=self.after)
                                    if nseq is not None:
                                        S.dma("sp", f"xl{tj}", self.X[:, tj, :],
                                              self.dram["x"][nseq, tj * 128:(tj + 1) * 128, :],
                                              writes=[self.Xb[tj]])
        self.barrier()

    def headnorm(self, ps_ap, psb, dst, gcol, qscale, tmp, tmpb, hsq, hsqb, hrs, hrsb):
        self.act(tmp[:], ps_ap, AF.Square, psb, [tmpb])
        self.S.op("dve", lambda e: e.tensor_reduce(hsq[:], tmp[:].rearrange("p (h d) -> p h d", d=64), AX.X,
                                                   ALU.add), [tmpb], [hsqb])
        self.act(hsq[:], hsq[:], AF.Sqrt, [hsqb], [hsqb], bias=EPS, scale=1.0 / 64)
        self.recip(hrs[:], hsq[:], [hsqb], [hrsb])
        p3 = ps_ap.rearrange("p (h d) -> p h d", d=64)
        t3 = tmp[:].rearrange("p (h d) -> p h d", d=64)
        self.stt("dve", t3, p3, qscale, hrs[:].unsqueeze(2).broadcast_to([128, 8, 64]), ALU.mult, ALU.mult,
                 psb + [hrsb, tmpb], [tmpb])
        gbc = self.roww[:, gcol:gcol + 64].unsqueeze(1).broadcast_to([128, 8, 64])
        return t3, gbc

    def fox(self):
        S = self.S
        HD = 70
        if getattr(self, "fresh_x", False):
            self.fresh_x = False
            for i in range(NT):
                self.x_final(i)
        self.stats_begin()
        with ExitStack() as sc:
            self.extra_slots(sc, 1)
            ka = self.sb(sc, "ka", [128, NT, 16, HD], BF16); kab = Buf("ka")
            va = self.sb(sc, "va", [128, NT, 16, 65], BF16); vab = Buf("va")
            wf = self.sb(sc, "wf", [128, 8, 16], BF16); wfb = Buf("wf")
            S.dma("sp", "wfl", wf[:], self.wd["kv_w_in"].rearrange("(c p) n -> p c n", p=128)[:, :, 2048:2064],
                  reads=[self.wdb["kv_w_in"]], writes=[wfb], after=self.after)
            HT = self.sb(sc, "HTx", [128, 8, 512], BF16); HTb = [Buf(f"HTx{i}") for i in range(4)]
            qa = self.sb(sc, "qa", [128, 4, 16, HD], BF16); qab = [Buf(f"qa{h}") for h in range(16)]
            csp = self.sb(sc, "csp", [128, 3, NT, 16], BF16); cspb = Buf("csp")
            tmp = [self.sb(sc, f"hn{i}", [128, 512], F32) for i in range(2)]; tmpb = [Buf(), Buf()]
            hsq2 = [self.sb(sc, f"hsq{i}", [128, 8], F32) for i in range(2)]; hsq2b = [Buf(), Buf()]
            hrs2 = [self.sb(sc, f"hrs{i}", [128, 8], F32) for i in range(2)]; hrs2b = [Buf(), Buf()]
            kvsc = ExitStack()
            spa = self.sb(kvsc, "spa", [128, NT, 16], F32); spab = Buf("spa")
            call = self.sb(kvsc, "call", [128, NT, 16], F32); callb = Buf("call")
            r1, r1b = spa, spab
            fsb = self.sb(kvsc, "fsb", [128, 16], F32); fsbb = Buf("fsb")
            self.memset("pool", va[:, :, :, 64:65], 1.0, [vab])
            self.memset("pool", ka[:, :, :, 67:70], 1.0, [kab])
            for h in range(16):
                self.memset("pool", qa[:, :, h, 64:67], 1.0, [qab[h]])
            for g in range(4):
                self.phase_rstd(4 * g, 4, src=(self.rd, self.rdb))
                ngen = self.norm_gen(4 * g, 4, CP_G["kv_g"], HT, HTb)
                next(ngen, None)
                for which in range(2):
                    for cg in range(2):
                        wt, wtb = self.wtile("kv_w_in", 0, 8, which * 1024 + cg * 512, 512)
                        for i in range(4):
                            next(ngen, None)
                            ti = 4 * g + i
                            tok = slice(i * 128, (i + 1) * 128)
                            pk, pkb = self.ps(1)
                            for kc in range(8):
                                self.mm(pk, HT[:, kc, tok], wt[:, kc, :], [wtb, HTb[i]], pkb,
                                        start=(kc == 0), stop=(kc == 7))
                            if which == 0:
                                k_ = (cg * 4 + i) % 2
                                t3, gbc = self.headnorm(pk, pkb, None, RW_GK, 1.0, tmp[k_], tmpb[k_], hsq2[k_],
                                                        hsq2b[k_], hrs2[k_], hrs2b[k_])
                                self.tt("dve", ka[:, ti, cg * 8:(cg + 1) * 8, 0:64], t3, gbc, ALU.mult,
                                        [tmpb[k_], self.cbb], [kab])
                            else:
                                self.cp("act", va[:, ti, cg * 8:(cg + 1) * 8, 0:64],
                                        pk.rearrange("p (h d) -> p h d", d=64), pkb, [vab])
                for i in range(4):
                    ti = 4 * g + i
                    tok = slice(i * 128, (i + 1) * 128)
                    pf, pfb = self.ps(1)
                    for kc in range(8):
                        self.mm(pf[:, 0:16], HT[:, kc, tok], wf[:, kc, :], [wfb, HTb[i]], pfb,
                                start=(kc == 0), stop=(kc == 7))
                    self.tt("dve", fsb[:], pf[:, 0:16], self.roww[:, RW_BF:RW_BF + 16], ALU.add,
                            pfb + [self.cbb], [fsbb])
                    self.act(fsb[:], fsb[:], AF.Exp, [fsbb], [fsbb], scale=-1.0)
                    self.act(spa[:, ti, :], fsb[:], AF.Ln, [fsbb], [spab], bias=1.0)
            for ti in range(NT):
                pc, pcb = self.ps(1)
                self.mm(pc[:, 0:16], self.trineg, spa[:, ti, :], [spab, self.cbb], pcb)
                if ti > 0:
                    self.mm(pc[:, 16:32], self.onesneg, spa[:, ti - 1, :], [spab, self.cbb], pcb)
                    if ti == 1:
                        self.cp("dve", fsb[:], pc[:, 16:32], pcb, [fsbb])
                    else:
                        self.tt("dve", fsb[:], fsb[:], pc[:, 16:32], ALU.add, pcb + [fsbb], [fsbb])
                    self.tt("dve", call[:, ti, :], pc[:, 0:16], fsb[:], ALU.add, pcb + [fsbb], [callb])
                else:
                    self.cp("dve", call[:, ti, :], pc[:, 0:16], pcb, [callb])
            self.cp("dve", csp[:, 0], call[:], [callb], [cspb])
            self.tt("dve", r1[:], call[:], csp[:, 0], ALU.subtract, [callb, cspb], [r1b])
            self.cp("dve", csp[:, 1], r1[:], [r1b], [cspb])
            self.tt("dve", r1[:], r1[:], csp[:, 1], ALU.subtract, [r1b, cspb], [r1b])
            self.cp("dve", csp[:, 2], r1[:], [r1b], [cspb])
            for k in range(3):
                self.ts("dve", ka[:, :, :, 64 + k], csp[:, k], -1.0, None, ALU.mult, None, [cspb], [kab])
            self.after = self.S.barrier()
            kvsc.close()
            kT = [self.sb(sc, f"kT{i}", [HD, T], BF16) for i in range(2)]; kTb = [Buf(), Buf()]
            qT = [self.sb(sc, f"qT{i}", [HD, 512], BF16) for i in range(2)]; qTb = [Buf(), Buf()]
            ptsall = self.sb(sc, "ptsall", [128, 3, 512], BF16)
            pts = [ptsall[:, i, :] for i in range(3)]; ptsb = [Buf() for _ in range(3)]
            rden = self.sb(sc, "rden", [128, 4], F32); rdenb = Buf("rden")
            a2v = ptsall[:, 0:2, :].rearrange("p a b -> p (a b)")
            a2bs = [ptsb[0], ptsb[1]]
            self.psn = 6
            self.pspos = 0
            pob = [self.psb[6], self.psb[7]]
            pot = [self.pst[3][:, 0:512], self.pst[3][:, 512:1024]]
            hcount = 0
            for g in range(4):
                ngen = self.norm_gen(4 * g, 4, CP_G["g_mix1"], HT, HTb)
                next(ngen, None)
                for cg in range(2):
                    wt, wtb = self.wtile("fox_w_in", 0, 8, cg * 512, 512)
                    for i in range(4):
                        next(ngen, None)
                        tok = slice(i * 128, (i + 1) * 128)
                        pq, pqb = self.ps(1)
                        for kc in range(8):
                            self.mm(pq, HT[:, kc, tok], wt[:, kc, :], [wtb, HTb[i]], pqb,
                                    start=(kc == 0), stop=(kc == 7))
                        k_ = (cg * 4 + i) % 2
                        t3, gbc = self.headnorm(pq, pqb, None, RW_GQ, 0.125, tmp[k_], tmpb[k_], hsq2[k_],
                                                hsq2b[k_], hrs2[k_], hrs2b[k_])
                        self.tt("dve", qa[:, i, cg * 8:(cg + 1) * 8, 0:64], t3, gbc, ALU.mult,
                                [tmpb[k_], self.cbb], qab[cg * 8:(cg + 1) * 8])
                for k in range(3):
                    self.cp("pool", qa[:, :, :, 67 + k], csp[:, k, 4 * g:4 * g + 4, :], [cspb], qab)
                nk = 4 * g + 4
                def prep(h, hb):
                    pt, ptb_ = self.ps(1)
                    ptb = pt.bitcast(BF16)
                    for i in range(4):
                        self.tr(ptb[0:HD, i * 128:(i + 1) * 128], qa[:, i, h, :], [qab[h]], ptb_)
                    self.cp("dve", qT[hb][:, :], ptb[0:HD, 0:512], ptb_, [qTb[hb]])
                    for j0 in range(0, nk, 4):
                        pt2, pt2b = self.ps(1)
                        pt2v = pt2.bitcast(BF16)
                        for jj in range(4):
                            self.tr(pt2v[0:HD, jj * 128:(jj + 1) * 128], ka[:, j0 + jj, h, :], [kab], pt2b)
                        self.cp("dve", kT[hb][:, j0 * 128:(j0 + 4) * 128], pt2v[0:HD, 0:512], pt2b, [kTb[hb]])

                def qk_exp(h, hb, j):
                    ks = slice(j * 128, (j + 1) * 128)
                    pS, pSb = self.ps(1)
                    i0 = max(0, j - 4 * g)
                    if j >= 4 * g:
                        dg = slice(i0 * 128, (i0 + 1) * 128)
                        self.mm(pS[:, dg], self.ident, self.maskneg, [self.cbb], pSb, start=True, stop=False)
                        self.mm(pS[:, dg], kT[hb][:, ks], qT[hb][:, dg], [kTb[hb], qTb[hb]], pSb,
                                start=False, stop=True)
                        if i0 < 3:
                            rest = slice((i0 + 1) * 128, 512)
                            self.mm(pS[:, rest], kT[hb][:, ks], qT[hb][:, rest], [kTb[hb], qTb[hb]], pSb)
                    else:
                        self.mm(pS, kT[hb][:, ks], qT[hb][:, :], [kTb[hb], qTb[hb]], pSb)
                    pp, ppb = pts[j % 3], ptsb[j % 3]
                    self.act(pp[:, i0 * 128:512], pS[:, i0 * 128:512], AF.Exp, pSb, [ppb])
                    return (j, i0, pp, ppb)

                def pv(h, hb, blk):
                    j, i0, pp, ppb = blk
                    po, pobuf = pot[hb], [pob[hb]]
                    for i in range(i0, 4):
                        self.mm(po[:, i * 128:i * 128 + 65], pp[:, i * 128:(i + 1) * 128], va[:, j, h, :],
                                [ppb, vab], pobuf, start=(j == 0 and i == 0), stop=(j == 4 * g + i))

                prep(0, hcount % 2)
                for h in range(16):
                    hb = hcount % 2
                    hcount += 1
                    if h + 1 < 16:
                        prep(h + 1, hcount % 2)
                    pendq = []
                    for j in range(nk):
                        pendq.append(qk_exp(h, hb, j))
                        if len(pendq) > 2:
                            pv(h, hb, pendq.pop(0))
                    for blk in pendq:
                        pv(h, hb, blk)
                    po = pot[hb]
                    pobuf = [pob[hb]]
                    po3 = po.rearrange("p (i c) -> p i c", c=128)
                    self.recip(rden[:].unsqueeze(2), po3[:, :, 64:65], pobuf, [rdenb])
                    self.tt("dve", qa[:, :, h, 0:64], po3[:, :, 0:64],
                            rden[:].unsqueeze(2).broadcast_to([128, 4, 64]), ALU.mult, pobuf + [rdenb], [qab[h]])
                wog = [self.wtile("fox_w_in", 0, 8, 1024 + cg * 512, 512) for cg in range(2)]
                for i in range(4):
                    tok = slice(i * 128, (i + 1) * 128)
                    for cg in range(2):
                        wt, wtb = wog[cg]
                        pg, pgb = self.ps(1)
                        for kc in range(8):
                            self.mm(pg, HT[:, kc, tok], wt[:, kc, :], [wtb, HTb[i]], pgb,
                                    start=(kc == 0), stop=(kc == 7))
                        k_ = (cg * 4 + i) % 2
                        self.act(tmp[k_][:], pg, AF.Sigmoid, pgb, [tmpb[k_]])
                        av = qa[:, i, cg * 8:(cg + 1) * 8, 0:64]
                        self.tt("dve", a2v[:, cg * 512:(cg + 1) * 512].rearrange("p (h d) -> p h d", d=64), av,
                                tmp[k_][:].rearrange("p (h d) -> p h d", d=64), ALU.mult,
                                [tmpb[k_]] + qab[cg * 8:(cg + 1) * 8], a2bs)
                    pt, ptb_ = self.ps(1)
                    ptb = pt.bitcast(BF16)
                    for kc in range(8):
                        self.tr(ptb[:, kc * 128:(kc + 1) * 128], a2v[:, kc * 128:(kc + 1) * 128], a2bs, ptb_)
                    self.cp("dve", HT[:, :, tok], ptb.rearrange("p (c t) -> p c t", t=128), ptb_, [HTb[i]])
                for cg in range(2):
                    wt, wtb = self.wtile("fox_w_out", 0, 8, cg * 512, 512)
                    for i in range(4):
                        ti = 4 * g + i
                        px, pxb = self.ps(1)
                        for kc in range(8):
                            self.mm(px, HT[:, kc, i * 128:(i + 1) * 128], wt[:, kc, :], [HTb[i], wtb], pxb,
                                    start=(kc == 0), stop=(kc == 7))
                        xs = self.X[:, ti, cg * 512:(cg + 1) * 512]
                        self.tt("dve", xs, xs, px, ALU.add, pxb + [self.Xb[ti]], [self.Xb[ti]])
                        if cg == 1:
                            self.x_final(ti)
            self.psn = 8
        self.barrier()

    def store_raw(self, seq, out):
        for i in range(NT):
            self.S.dma("sp", f"xs{i}", out[seq, i * 128:(i + 1) * 128, :], self.X[:, i, :], reads=[self.Xb[i]])

    def final_norm_store(self, seq, out, nseq=None):
        S = self.S
        ss, sq, rs = self.nss, self.nsq, self.nrs
        with ExitStack() as sc:
            NOB = 6
            ob = [self.sb(sc, f"ob{i}", [128, 1024], F32) for i in range(NOB)]; obb = [Buf() for _ in range(NOB)]
            gfin = self.sb(sc, "gfin", [128, 1024], F32); gfinb = Buf("gfin")
            S.dma("sp", "cst", gfin[:], self.dram["gfin"].partition_broadcast(128), writes=[gfinb], after=self.after)
            self.phase_rstd()
            for i in range(NT):
                o, obf = ob[i % NOB], obb[i % NOB]
                self.stt("dve", o[:], self.X[:, i, :], rs[:, i:i + 1], gfin[:],
                         ALU.mult, ALU.mult, [self.Xb[i], self.nrsb, gfinb], [obf])
                S.dma("sp", f"ost{i % NOB}", out[seq, i * 128:(i + 1) * 128, :], o[:], reads=[obf], after=self.after)
                if nseq is not None:
                    S.dma("sp", f"xl{i}", self.X[:, i, :], self.dram["x"][nseq, i * 128:(i + 1) * 128, :],
                          writes=[self.Xb[i]])
        self.barrier()

    def finish(self):
        S = self.S
        for key in list(S.sems.keys()):
            if key.startswith("ost") or key.startswith("xs"):
                S._need("sp", (key, S.cnt[key]), False)
        for e in ("pe", "act", "dve", "pool"):
            if S.cnt[e] > 0:
                S._need("sp", (e, S.cnt[e]), False)
        S.emit()


def build(nseq, layers=(0, 1), final=True):
    nc = bass.Bass("TRN2", target_bir_lowering=False)
    dram = {}
    dram["x"] = nc.dram_tensor("x", [nseq, T, D], F32, kind="ExternalInput").ap()
    dram["p"] = nc.dram_tensor("p", [2, nseq, T, 256], F32, kind="ExternalInput").ap()
    for name, K, N in WNAMES:
        dram[name] = nc.dram_tensor(name, [K, N], F32, kind="ExternalInput").ap()
    dram["cpp"] = nc.dram_tensor("cpp", [128, NCP], F32, kind="ExternalInput").ap()
    dram["roww"] = nc.dram_tensor("roww", [1, NRW], F32, kind="ExternalInput").ap()
    dram["gfin"] = nc.dram_tensor("gfin", [1, 1024], F32, kind="ExternalInput").ap()
    dram["gonrow"] = nc.dram_tensor("gonrow", [1, 1024], F32, kind="ExternalInput").ap()
    dram["cf32"] = nc.dram_tensor("cf32", [128, 640], F32, kind="ExternalInput").ap()
    dram["cbf"] = nc.dram_tensor("cbf", [128, 256], BF16, kind="ExternalInput").ap()
    out = nc.dram_tensor("out", [nseq, T, D], F32, kind="ExternalOutput").ap()
    with ExitStack() as es:
        kb = KB(nc, es, nseq)
        kb.setup(dram)
        for s in range(nseq):
            kb.load_x(s)
            if 0 in layers:
                kb.gla()
                kb.ffn(0)
                kb.ple(0, s)
            fused_tail = final and (1 in layers)
            nseq_ = s + 1 if s + 1 < nseq else None
            if 1 in layers:
                kb.fox()
                kb.ffn(1)
                if fused_tail:
                    kb.ple(1, s, fuse=(out, nseq_))
                    kb.preloaded = nseq_ is not None
                else:
                    kb.ple(1, s)
            if fused_tail:
                pass
            elif final:
                kb.final_norm_store(s, out, nseq_)
                kb.preloaded = nseq_ is not None
            else:
                kb.store_raw(s, out)
        kb.finish()
    return nc


def run_layers(x, p, shared, layers, final, ncores=NCORES):
    B = x.shape[0]
    nseq = B // ncores
    nc = build(nseq, layers, final)
    in_maps = []
    for c in range(ncores):
        m = dict(shared)
        m["x"] = np.ascontiguousarray(x[c * nseq:(c + 1) * nseq])
        m["p"] = np.ascontiguousarray(p[:, c * nseq:(c + 1) * nseq])
        in_maps.append(m)
    res = run_bass_kernel_spmd(nc, in_maps, core_ids=list(range(ncores)))
    return np.concatenate([np.asarray(r["out"]) for r in res.results], axis=0)


MODE = "fused"


def kernel(**inputs):
    inp = {k: np.asarray(v) for k, v in inputs.items()}
    shared = _prep_shared(inp)
    x = np.ascontiguousarray(inp["x"], dtype=np.float32)
    p = np.ascontiguousarray(inp["p"], dtype=np.float32)
    if MODE == "fused":
        out = run_layers(x, p, shared, (0, 1), True)
    else:
        x1 = run_layers(x, p, shared, (0,), False)
        out = run_layers(x1, p, shared, (1,), True)
    return out.astype(np.float32)
```

```python
from contextlib import ExitStack
import numpy as np
import ml_dtypes
import concourse.bass as bass
import concourse.mybir as mybir
from concourse.bass_utils import run_bass_kernel_spmd

F32 = mybir.dt.float32
BF16 = mybir.dt.bfloat16
AF = mybir.ActivationFunctionType
ALU = mybir.AluOpType
AX = mybir.AxisListType

D = 1024
T = 2048
NT = 16
DFF = 2816
NFC = 22
EPS = 1e-6
NCORES = 8


class Buf:
    __slots__ = ("name", "w", "r")

    def __init__(self, name=""):
        self.name = name
        self.w = None
        self.r = {}


class Sched:
    ENG = ("pe", "act", "dve", "pool", "sp")

    def __init__(self, nc, es):
        self.nc = nc
        self.es = es
        self.streams = {k: [] for k in self.ENG}
        self.sems = {}
        self.cnt = {}
        self.waited = {k: {} for k in self.ENG}
        for k in self.ENG:
            self._mksem(k)

    def _mksem(self, key):
        self.sems[key] = self.es.enter_context(self.nc.semaphore("s_" + key))
        self.cnt[key] = 0

    def _need(self, eng, ev, same_ok):
        if ev is None:
            return
        key, val = ev
        if same_ok and key == eng:
            return
        if self.waited[eng].get(key, 0) >= val:
            return
        self.waited[eng][key] = val
        self.streams[eng].append(("w", key, val))

    def _deps(self, eng, reads, writes, is_dma):
        for b in reads:
            self._need(eng, b.w, False)
        for b in writes:
            self._need(eng, b.w, not is_dma)
            for k, v in b.r.items():
                self._need(eng, (k, v), not is_dma)

    def _mark(self, ev, reads, writes):
        for b in writes:
            b.w = ev
            b.r = {}
        for b in reads:
            if b.r.get(ev[0], 0) < ev[1]:
                b.r[ev[0]] = ev[1]

    def op(self, eng, fn, reads=(), writes=()):
        self._deps(eng, reads, writes, False)
        self.cnt[eng] += 1
        ev = (eng, self.cnt[eng])
        self.streams[eng].append(("o", fn, eng, 1))
        self._mark(ev, reads, writes)
        return ev

    def dma(self, eng, semkey, out, in_, reads=(), writes=(), after=()):
        if semkey not in self.sems:
            self._mksem(semkey)
        for ev in after:
            self._need(eng, ev, False)
        self._deps(eng, reads, writes, True)
        self.cnt[semkey] += 16
        ev = (semkey, self.cnt[semkey])
        self.streams[eng].append(("o", lambda e: e.dma_start(out=out, in_=in_), semkey, 16))
        self._mark(ev, reads, writes)
        return ev

    def barrier(self):
        evs = [(k, self.cnt[k]) for k in ("pe", "act", "dve", "pool") if self.cnt[k] > 0]
        for e in ("pe", "act", "dve", "pool"):
            for ev in evs:
                if ev[0] != e:
                    self._need(e, ev, False)
        return evs

    def emit(self):
        nc = self.nc
        with nc.Block() as block:
            def mk(key):
                def body(e):
                    for it in self.streams[key]:
                        if it[0] == "w":
                            e.wait_ge(self.sems[it[1]], it[2])
                        else:
                            it[1](e).then_inc(self.sems[it[2]], it[3])
                return body
            block.tensor(mk("pe"))
            block.scalar(mk("act"))
            block.vector(mk("dve"))
            block.gpsimd(mk("pool"))
            block.sync(mk("sp"))


WNAMES = [("gla_w_in", 1024, 3088), ("gla_w_out", 1024, 1024), ("kv_w_in", 1024, 2064),
          ("fox_w_in", 1024, 2048), ("fox_w_out", 1024, 1024),
          ("ffn_w_up0", 1024, 5632), ("ffn_w_up1", 1024, 5632),
          ("ffn_w_down0", 2816, 1024), ("ffn_w_down1", 2816, 1024),
          ("ple_w_gate0", 1024, 1024), ("ple_w_gate1", 1024, 1024),
          ("ple_w_proj0", 256, 1024), ("ple_w_proj1", 256, 1024),
          ("gk2aug", 17, 512), ("bgate", 2, 1024)]

CP_G = {"g_mix0": 0, "g_ffn0": 8, "g_ple0": 16, "kv_g": 24, "g_mix1": 32, "g_ffn1": 40, "g_ple1": 48,
        "gon": 56}
CP_WDW = 64
CP_BDW = 64 + 264
NCP = 64 + 264 + 88
RW_GFIN = 0
RW_GQ = 0
RW_GK = 64
RW_BF = 128
NRW = 144


def _host_consts():
    s = np.arange(128)[:, None]
    c = np.arange(128)[None, :]
    le = (s <= c).astype(np.float32)
    gt = (s > c).astype(np.float32)
    cf = np.concatenate([le, -le / 16.0, -gt / 16.0, -le, -np.ones((128, 128), np.float32)], axis=1)
    cb = np.concatenate([np.eye(128, dtype=np.float32), -30000.0 * gt], axis=1).astype(ml_dtypes.bfloat16)
    return np.ascontiguousarray(cf, np.float32), np.ascontiguousarray(cb)


def _fm(v):
    v = np.asarray(v, np.float32)
    return np.ascontiguousarray(v.reshape(-1, 128).T)


def _prep_shared(inp):
    sh = {}
    sh["gla_w_in"] = inp["gla_w_in"][0]
    sh["gla_w_out"] = inp["gla_w_out"][0]
    sh["kv_w_in"] = inp["kv_w_in"]
    sh["fox_w_in"] = inp["fox_w_in"][0]
    sh["fox_w_out"] = inp["fox_w_out"][0]
    for l in range(2):
        sh[f"ffn_w_up{l}"] = inp["ffn_w_up"][l]
        sh[f"ffn_w_down{l}"] = inp["ffn_w_down"][l]
        sh[f"ple_w_gate{l}"] = inp["ple_w_gate"][l]
        sh[f"ple_w_proj{l}"] = inp["ple_w_proj"][l]
    sh["gk2aug"] = np.concatenate([inp["gla_w_gk2"][0], inp["gla_b_gk2"][0][None, :]], axis=0)
    sh["bgate"] = inp["ple_b_gate"]
    cp = np.zeros((128, NCP), np.float32)
    cp[:, 0:8] = _fm(inp["g_mix"][0]); cp[:, 8:16] = _fm(inp["g_ffn"][0]); cp[:, 16:24] = _fm(inp["g_ple"][0])
    cp[:, 24:32] = _fm(inp["kv_g_norm"]); cp[:, 32:40] = _fm(inp["g_mix"][1])
    cp[:, 40:48] = _fm(inp["g_ffn"][1]); cp[:, 48:56] = _fm(inp["g_ple"][1])
    cp[:, 56:64] = _fm(np.tile(inp["gla_g_onorm"][0], 4))
    for l in range(2):
        for tap in range(3):
            o = CP_WDW + (l * 3 + tap) * 44
            cp[:, o:o + 44] = _fm(inp["ffn_w_dw"][l, tap])
        o = CP_BDW + l * 44
        cp[:, o:o + 44] = _fm(inp["ffn_b_dw"][l])
    sh["cpp"] = cp
    rw = np.zeros((1, NRW), np.float32)
    rw[0, RW_GQ:RW_GQ + 64] = inp["fox_g_qnorm"][0]
    rw[0, RW_GK:RW_GK + 64] = inp["kv_g_knorm"]
    rw[0, RW_BF:RW_BF + 16] = inp["kv_b_f"]
    sh["roww"] = rw
    sh["gfin"] = np.asarray(inp["g_final"], np.float32).reshape(1, 1024)
    sh["gonrow"] = np.tile(np.asarray(inp["gla_g_onorm"][0], np.float32), 4).reshape(1, 1024)
    cf, cb = _host_consts()
    sh["cf32"] = cf
    sh["cbf"] = cb
    return {k: np.ascontiguousarray(np.asarray(v)) for k, v in sh.items()}


class KB:
    def __init__(self, nc, es, nseq):
        self.nc = nc
        self.es = es
        self.nseq = nseq
        self.S = Sched(nc, es)
        self.uid = 0
        self.after = []
        self.pst = [es.enter_context(nc.psum_tensor(f"ps{i}", [128, 1024], F32)) for i in range(4)]
        self.psb = [Buf(f"psb{i}") for i in range(8)]
        self.pspos = 0
        self.psn = 8

    def sb(self, scope, name, shape, dt):
        self.uid += 1
        return scope.enter_context(self.nc.sbuf_tensor(f"{name}_{self.uid}", shape, dt))

    def ps(self, n=1):
        if n == 2 and self.pspos % 2:
            self.pspos += 1
        b = self.pspos % self.psn
        if b + n > self.psn:
            self.pspos += self.psn - b
            b = 0
        self.pspos += n
        t = self.pst[b // 2]
        off = (b % 2) * 512
        return t[:, off:off + 512 * n], self.psb[b:b + n]

    def mm(self, out, lhsT, rhs, r, w, start=True, stop=True):
        self.S.op("pe", lambda e: e.matmul(out, lhsT, rhs, start=start, stop=stop), r, w)

    def tr(self, out, in_, r, w):
        kp = in_.shape[0]
        idt = self.ident[0:kp, 0:kp]
        self.S.op("pe", lambda e: e.transpose(out, in_, idt), list(r) + [self.cbb], w)

    def act(self, out, in_, func, r, w, bias=None, scale=None, accum=None):
        kw = {}
        if bias is not None:
            kw["bias"] = bias
        if scale is not None:
            kw["scale"] = scale
        if accum is not None:
            kw["accum_out"] = accum
        self.S.op("act", lambda e: e.activation(out, in_, func, **kw), r, w)

    def amul(self, out, in_, mul, r, w):
        self.S.op("act", lambda e: e.mul(out, in_, mul), r, w)

    def tt(self, eng, out, in0, in1, op, r, w):
        self.S.op(eng, lambda e: e.tensor_tensor(out, in0, in1, op), r, w)

    def stt(self, eng, out, in0, scalar, in1, op0, op1, r, w):
        self.S.op(eng, lambda e: e.scalar_tensor_tensor(out, in0, scalar, in1, op0, op1), r, w)

    def ts(self, eng, out, in0, s1, s2, op0, op1, r, w):
        if s2 is None:
            self.S.op(eng, lambda e: e.tensor_scalar(out, in0, s1, None, op0), r, w)
        else:
            self.S.op(eng, lambda e: e.tensor_scalar(out, in0, s1, s2, op0, op1), r, w)

    def cp(self, eng, out, in_, r, w):
        if eng == "act":
            self.S.op("act", lambda e: e.copy(out, in_), r, w)
        else:
            self.S.op(eng, lambda e: e.tensor_copy(out, in_), r, w)

    def memset(self, eng, ap, val, w):
        self.S.op(eng, lambda e: e.memset(ap, val), (), w)

    def recip(self, out, in_, r, w):
        self.S.op("dve", lambda e: e.reciprocal(out, in_), r, w)

    def barrier(self):
        self.after = self.S.barrier()
        self.wslots = self.wslots[:2]
        self.wslotb = self.wslotb[:2]
        self.wpos = 0

    def wtile(self, wname, kc0, nkc, c0, ncols):
        i = self.wpos % len(self.wslots)
        self.wpos += 1
        slot, b = self.wslots[i], self.wslotb[i]
        src = self.wd[wname].rearrange("(c p) n -> p c n", p=128)[:, kc0:kc0 + nkc, c0:c0 + ncols]
        self.S.dma("sp", f"wr{i}", slot[:, 0:nkc, 0:ncols], src, reads=[self.wdb[wname]], writes=[b],
                   after=(self.after if i >= 2 else ()))
        return slot, b

    def extra_slots(self, scope, n):
        self.wslots = self.wslots[:2] + [self.sb(scope, f"wsx{i}", [128, 8, 512], BF16) for i in range(n)]
        self.wslotb = self.wslotb[:2] + [Buf(f"wsx{i}") for i in range(n)]
        self.wpos = 0

    def setup(self, dram):
        nc, es, S = self.nc, self.es, self.S
        self.dram = dram
        self.wd, self.wdb = {}, {}
        for name, K, N in WNAMES:
            self.wd[name] = nc.dram_tensor("wb_" + name, [K, N], BF16, kind="Internal").ap()
            self.wdb[name] = Buf("wd_" + name)
        order = ["gk2aug", "bgate", "gla_w_in", "gla_w_out", "ffn_w_up0", "ffn_w_down0", "ple_w_gate0",
                 "ple_w_proj0", "kv_w_in", "fox_w_in", "fox_w_out", "ffn_w_up1", "ffn_w_down1",
                 "ple_w_gate1", "ple_w_proj1"]
        dims = {n: (K, N) for n, K, N in WNAMES}
        self.wdb["gla_w_in_qk"] = Buf("wd_gla_w_in_qk")
        for (c0, c1) in ((0, 1024), (3072, 3088)):
            for r0 in range(0, 1024, 256):
                S.dma("pool", "cv_gla_qk", self.wd["gla_w_in"][r0:r0 + 256, c0:c1],
                      dram["gla_w_in"][r0:r0 + 256, c0:c1], writes=[self.wdb["gla_w_in_qk"]])
        for name in order:
            K, N = dims[name]
            if name == "gla_w_in":
                for r0 in range(0, 1024, 128):
                    S.dma("pool", "cv_" + name, self.wd[name][r0:r0 + 128, 1024:3072],
                          dram[name][r0:r0 + 128, 1024:3072], writes=[self.wdb[name]])
                continue
            r0 = 0
            while r0 < K:
                r1 = min(K, r0 + 128)
                S.dma("pool", "cv_" + name, self.wd[name][r0:r1, :], dram[name][r0:r1, :],
                      writes=[self.wdb[name]])
                r0 = r1
        self.cpp = self.sb(es, "cpp", [128, NCP], F32)
        self.roww = self.sb(es, "roww", [128, NRW], F32)
        self.cf32 = self.sb(es, "cf32", [128, 640], F32)
        self.cbf = self.sb(es, "cbf", [128, 256], BF16)
        self.bgate = self.sb(es, "bgate", [1, 2, 1024], BF16)
        self.ones = self.sb(es, "ones", [1, 128], BF16)
        self.cbb = Buf("consts")
        S.dma("sp", "cst", self.cpp[:], dram["cpp"], writes=[self.cbb])
        S.dma("sp", "cst", self.roww[:], dram["roww"].partition_broadcast(128), writes=[self.cbb])
        S.dma("sp", "cst", self.cf32[:], dram["cf32"], writes=[self.cbb])
        S.dma("sp", "cst", self.cbf[:], dram["cbf"], writes=[self.cbb])
        S.dma("sp", "cst", self.bgate[:], self.wd["bgate"].rearrange("(o l) n -> o l n", o=1),
              reads=[self.wdb["bgate"]], writes=[self.cbb])
        self.memset("dve", self.ones[:], 1.0, [self.cbb])
        self.ident = self.cbf[:, 0:128]
        self.maskneg = self.cbf[:, 128:256]
        self.mask01 = self.cf32[:, 0:128]
        self.mcum = self.cf32[:, 128:256]
        self.mrev = self.cf32[:, 256:384]
        self.trineg = self.cf32[:, 384:512]
        self.onesneg = self.cf32[:, 512:640]
        self.X = self.sb(es, "X", [128, NT, D], F32)
        self.Xb = [Buf(f"X{i}") for i in range(NT)]
        self.wslots = [self.sb(es, f"wslot{i}", [128, 8, 512], BF16) for i in range(2)]
        self.wslotb = [Buf(f"wslot{i}") for i in range(2)]
        self.wpos = 0
        self.pbF = self.sb(es, "pbF", [128, 4, 256], BF16)
        self.pbFb = Buf("pbF")
        self.nss2 = [self.sb(es, f"nss{i}", [128, 16], F32) for i in range(2)]
        self.nss2b = [Buf(f"nss{i}") for i in range(2)]
        self.sidx = 0
        self.nss, self.nssb = self.nss2[0], self.nss2b[0]
        self.nsq = self.sb(es, "nsq", [128, 16], F32); self.nsqb = Buf("nsq")
        self.nrs = self.sb(es, "nrs", [128, 16], F32); self.nrsb = Buf("nrs")
        self.xn2 = [self.sb(es, f"xn{i}", [128, 1024], BF16) for i in range(2)]
        self.xn2b = [Buf(f"xn{i}") for i in range(2)]
        self.junk, self.junkb = self.xn2[0], self.xn2b[0]

    def load_x(self, seq):
        self.cur_seq = seq
        self.stats_begin()
        if getattr(self, "preloaded", False):
            self.preloaded = False
        else:
            for i in range(NT):
                self.S.dma("sp", f"xl{i}", self.X[:, i, :], self.dram["x"][seq, i * 128:(i + 1) * 128, :],
                           writes=[self.Xb[i]])
        self.fresh_x = True

    def stats_begin(self):
        self.rd, self.rdb = self.nss, self.nssb
        self.sidx ^= 1
        self.nss, self.nssb = self.nss2[self.sidx], self.nss2b[self.sidx]
        self.memset("dve", self.nss[:], 0.0, [self.nssb])

    def x_final(self, ti):
        j = self.xn2[ti % 2]
        self.act(j[:], self.X[:, ti, :], AF.Square, [self.Xb[ti], self.nssb], [self.xn2b[ti % 2], self.nssb],
                 accum=self.nss[:, ti:ti + 1])

    def phase_rstd(self, t0=0, nt=NT, src=None):
        ss, ssb = src if src is not None else (self.nss, self.nssb)
        c = slice(t0, t0 + nt)
        self.act(self.nsq[:, c], ss[:, c], AF.Sqrt, [ssb], [self.nsqb], bias=EPS, scale=1.0 / D)
        self.recip(self.nrs[:, c], self.nsq[:, c], [self.nsqb], [self.nrsb])

    def norm_gen(self, t0, nt, gcol, HT, HTb):
        rs, rsb = self.nrs, self.nrsb
        htb = HTb if isinstance(HTb, list) else [HTb] * nt
        gbc = self.cpp[:, gcol:gcol + 8].unsqueeze(2).broadcast_to([128, 8, 128])
        for i in range(nt):
            xn, xnb = self.xn2[i % 2], self.xn2b[i % 2]
            self.amul(xn[:], self.X[:, t0 + i, :], rs[:, t0 + i:t0 + i + 1], [self.Xb[t0 + i], rsb], [xnb])
            pt, pb = self.ps(1)
            ptb = pt.bitcast(BF16)
            for kc in range(8):
                self.tr(ptb[:, kc * 128:(kc + 1) * 128], xn[:, kc * 128:(kc + 1) * 128], [xnb], pb)
            self.tt("dve", HT[:, :, i * 128:(i + 1) * 128], ptb.rearrange("p (c t) -> p c t", t=128), gbc,
                    ALU.mult, pb + [self.cbb], [htb[i]])
            yield i

    def norm_group(self, t0, nt, gcol, HT, HTb):
        for _ in self.norm_gen(t0, nt, gcol, HT, HTb):
            pass

    def gla(self):
        S = self.S
        fresh = getattr(self, "fresh_x", False)
        self.fresh_x = False
        if fresh:
            for i in range(8):
                self.x_final(i)
        self.stats_begin()
        NQ = 1040
        with ExitStack() as sc:
            self.extra_slots(sc, 2)
            win = self.sb(sc, "win", [128, 8, NQ], BF16); winb = Buf("win")
            wsrc = self.wd["gla_w_in"].rearrange("(c p) n -> p c n", p=128)
            S.dma("sp", "winl", win[:, :, 0:1024], wsrc[:, :, 0:1024], reads=[self.wdb["gla_w_in_qk"]],
                  writes=[winb], after=self.after)
            S.dma("sp", "winl", win[:, :, 1024:1040], wsrc[:, :, 3072:3088], reads=[self.wdb["gla_w_in_qk"]],
                  writes=[winb], after=self.after)
            gk2 = self.sb(sc, "gk2", [32, 512], BF16); gk2b = Buf("gk2")
            S.dma("sp", "gk2l", gk2[0:17, :], self.wd["gk2aug"], reads=[self.wdb["gk2aug"]], writes=[gk2b],
                  after=self.after)
            HT = self.sb(sc, "HT", [128, 8, 512], BF16); HTb = [Buf(f"HT{i}") for i in range(4)]
            AT, ATb = HT, HTb
            gr = self.sb(sc, "gr", [32, 512], BF16); grb = Buf("gr")
            sp = [self.sb(sc, f"sp{i}", [128, 512], F32) for i in range(2)]; spb = [Buf(), Buf()]
            eq = self.sb(sc, "eq", [128, 4, 512], F32); eqb = Buf("eq")
            ek = self.sb(sc, "ek", [128, 4, 512], F32); ekb = Buf("ek")
            een = [self.sb(sc, f"een{i}", [128, 512], F32) for i in range(2)]; eenb = [Buf(), Buf()]
            qin = self.sb(sc, "qin", [128, 4, 512], BF16); qinb = Buf("qin")
            kin = self.sb(sc, "kin", [128, 4, 512], BF16); kinb = Buf("kin")
            kend = [self.sb(sc, f"kend{i}", [128, 512], BF16) for i in range(4)]; kendb = [Buf() for _ in range(4)]
            vg = self.sb(sc, "vg", [128, 4, 1024], BF16); vgb = [Buf() for _ in range(4)]
            sgg = self.sb(sc, "sgg", [128, 4, 1024], BF16); sggb = [Buf() for _ in range(4)]
            attm = self.sb(sc, "attm", [128, 4, 128], BF16); attmb = Buf("attm")
            St = self.sb(sc, "St", [128, 4, 256], F32); Stb = Buf("St")
            Sbf = self.sb(sc, "Sbf", [128, 4, 256], BF16); Sbfb = Buf("Sbf")
            osq = self.sb(sc, "osq", [128, 4], F32); osqb = Buf("osq")
            ors = self.sb(sc, "ors", [128, 4], F32); orsb = Buf("ors")
            ojk = self.sb(sc, "ojk", [128, 256], BF16); ojkb = Buf("ojk")
            aa = [self.sb(sc, f"aa{i}", [128, 1024], BF16) for i in range(2)]; aab = [Buf(), Buf()]
            gonrow = self.sb(sc, "gonrow", [128, 1024], F32); gonrowb = Buf("gonrow")
            S.dma("sp", "gonl", gonrow[:], self.dram["gonrow"].partition_broadcast(128), writes=[gonrowb],
                  after=self.after)
            sgt, sgtb = een, eenb

            self.memset("dve", gr[:], 1.0, [grb])
            gonbc = self.cpp[:, CP_G["gon"]:CP_G["gon"] + 8].unsqueeze(2).broadcast_to([128, 8, 128])
            m01 = self.mask01.unsqueeze(1).broadcast_to([128, 4, 128])
            pend_final = []
            for g in range(4):
                self.phase_rstd(4 * g, 4, src=(self.rd, self.rdb))
                lazy = []
                if fresh and g + 2 < 4:
                    lazy = list(range(4 * (g + 2), 4 * (g + 3)))
                if False:
                    sv = (self.nss, self.nssb)
                    self.nss, self.nssb = self.rd, self.rdb
                    for i in range(4 * (g + 2), 4 * (g + 3)):
                        self.x_final(i)
                    self.nss, self.nssb = sv
                ngen = self.norm_gen(4 * g, 4, CP_G["g_mix0"], HT, HTb)
                next(ngen, None)
                for which in range(2):
                    for cg in range(2):
                        wt, wtb = self.wtile("gla_w_in", 0, 8, 1024 + which * 1024 + cg * 512, 512)
                        for i in range(4):
                            next(ngen, None)
                            tok = slice(i * 128, (i + 1) * 128)
                            pv, pvb = self.ps(1)
                            for kc in range(8):
                                self.mm(pv, HT[:, kc, tok], wt[:, kc, :], [wtb, HTb[i]], pvb,
                                        start=(kc == 0), stop=(kc == 7))
                            if which == 0:
                                self.cp("act", vg[:, i, cg * 512:(cg + 1) * 512], pv, pvb, [vgb[i]])
                            else:
                                k_ = (cg * 4 + i) % 2
                                self.act(sgt[k_][:], pv, AF.Silu, pvb, [sgtb[k_]])
                                self.tt("dve", sgg[:, i, cg * 512:(cg + 1) * 512], sgt[k_][:],
                                        gonrow[:, cg * 512:(cg + 1) * 512], ALU.mult, [sgtb[k_], gonrowb], [sggb[i]])
                        if which == 0 and cg == 0:
                            for ti_ in pend_final:
                                self.x_final(ti_)
                            pend_final = []
                            if lazy:
                                sv = (self.nss, self.nssb)
                                self.nss, self.nssb = self.rd, self.rdb
                                for ti_ in lazy:
                                    self.x_final(ti_)
                                self.nss, self.nssb = sv
                pg, pgb = self.ps(1)
                for kc in range(8):
                    self.mm(pg[0:16, :], win[:, kc, 1024:1040], HT[:, kc, :], [winb] + HTb, pgb,
                            start=(kc == 0), stop=(kc == 7))
                self.cp("dve", gr[0:16, :], pg[0:16, :], pgb, [grb])
                for i in range(4):
                    tok = slice(i * 128, (i + 1) * 128)
                    spt, sptb = sp[i % 2], spb[i % 2]
                    pz, pzb = self.ps(1)
                    self.mm(pz, gr[0:17, tok], gk2[0:17, :], [grb, gk2b], pzb)
                    self.act(spt[:], pz, AF.Exp, pzb, [sptb], scale=-1.0)
                    self.act(spt[:], spt[:], AF.Ln, [sptb], [sptb], bias=1.0)
                    pbt, pbtb = self.ps(1)
                    for h in range(4):
                        self.mm(pbt[:, h * 128:(h + 1) * 128], spt[:, h * 128:(h + 1) * 128], self.mcum,
                                [sptb, self.cbb], pbtb)
                    pb3 = pbt.rearrange("p (h c) -> p h c", c=128)
                    self.act(eq[:, :, tok], pb3, AF.Exp, pbtb, [eqb])
                    self.act(ek[:, :, tok], pb3, AF.Exp, pbtb, [ekb], scale=-1.0)
                    prv, prvb = self.ps(1)
                    self.mm(prv, self.mrev, spt[:], [sptb, self.cbb], prvb)
                    self.act(een[i % 2][:], prv, AF.Exp, prvb, [eenb[i % 2]])
                    pk2, pk2b = self.ps(1)
                    for kc in range(8):
                        self.mm(pk2, HT[:, kc, tok], win[:, kc, 512:1024], [winb, HTb[i]], pk2b,
                                start=(kc == 0), stop=(kc == 7))
                    self.tt("dve", kend[i][:], pk2, een[i % 2][:], ALU.mult, pk2b + [eenb[i % 2]], [kendb[i]])
                for h in range(4):
                    pq, pqb = self.ps(1)
                    for kc in range(8):
                        self.mm(pq, win[:, kc, h * 128:(h + 1) * 128], HT[:, kc, :], [winb] + HTb, pqb,
                                start=(kc == 0), stop=(kc == 7))
                    self.stt("dve", qin[:, h, :], pq, 128.0 ** -0.5, eq[:, h, :], ALU.mult, ALU.mult,
                             pqb + [eqb], [qinb])
                    pk, pkb = self.ps(1)
                    for kc in range(8):
                        self.mm(pk, win[:, kc, 512 + h * 128:512 + (h + 1) * 128], HT[:, kc, :], [winb] + HTb,
                                pkb, start=(kc == 0), stop=(kc == 7))
                    self.tt("dve", kin[:, h, :], pk, ek[:, h, :], ALU.mult, pkb + [ekb], [kinb])
                pendE = None

                def stageE(pe):
                    a_, ab_, tok_, ie_ = pe
                    pt, ptb_ = self.ps(1)
                    ptb = pt.bitcast(BF16)
                    for kc in range(8):
                        self.tr(ptb[:, kc * 128:(kc + 1) * 128], a_[:, kc * 128:(kc + 1) * 128], [ab_], ptb_)
                    self.cp("act", AT[:, :, tok_], ptb.rearrange("p (c t) -> p c t", t=128), ptb_, [ATb[ie_]])

                for i in range(4):
                    ti = 4 * g + i
                    tok = slice(i * 128, (i + 1) * 128)
                    pa, pab = self.ps(1)
                    for h in range(4):
                        self.mm(pa[:, h * 128:(h + 1) * 128], kin[:, h, tok], qin[:, h, tok], [kinb, qinb], pab)
                    self.tt("dve", attm[:], pa.rearrange("p (h c) -> p h c", c=128), m01, ALU.mult,
                            pab + [self.cbb], [attmb])
                    if ti < NT - 1:
                        pS, pSb = self.ps(2)
                        for h in range(4):
                            hs = slice(h * 256, (h + 1) * 256)
                            self.mm(pS[:, hs], kend[i][:, h * 128:(h + 1) * 128], vg[:, i, hs],
                                    [kendb[i], vgb[i]], pSb)
                    pO, pOb = self.ps(2)
                    for h in range(4):
                        hs = slice(h * 256, (h + 1) * 256)
                        self.mm(pO[:, hs], attm[:, h, :], vg[:, i, hs], [attmb, vgb[i]], pOb, start=True,
                                stop=(ti == 0))
                        if ti > 0:
                            self.mm(pO[:, hs], qin[:, h, tok], Sbf[:, h, :], [qinb, Sbfb], pOb,
                                    start=False, stop=True)
                    if ti < NT - 1:
                        for h in range(4):
                            hs = slice(h * 256, (h + 1) * 256)
                            if ti == 0:
                                self.cp("dve", St[:, h, :], pS[:, hs], pSb, [Stb])
                            else:
                                dcol = eq[:, h, i * 128 + 127:i * 128 + 128]
                                self.stt("dve", St[:, h, :], St[:, h, :], dcol, pS[:, hs], ALU.mult, ALU.add,
                                         pSb + [Stb, eqb], [Stb])
                        self.cp("act", Sbf[:], St[:], [Stb], [Sbfb])
                    self.memset("dve", osq[:], 0.0, [osqb])
                    for h in range(4):
                        hs = slice(h * 256, (h + 1) * 256)
                        self.act(ojk[:], pO[:, hs], AF.Square, pOb + [osqb], [ojkb, osqb], accum=osq[:, h:h + 1])
                    self.act(osq[:], osq[:], AF.Sqrt, [osqb], [osqb], bias=EPS, scale=1.0 / 256)
                    self.recip(ors[:], osq[:], [osqb], [orsb])
                    a, ab = aa[i % 2], aab[i % 2]
                    for h in range(4):
                        hs = slice(h * 256, (h + 1) * 256)
                        self.stt("dve", a[:, hs], pO[:, hs], ors[:, h:h + 1], sgg[:, i, hs], ALU.mult, ALU.mult,
                                 pOb + [orsb, sggb[i]], [ab])
                    if pendE is not None:
                        stageE(pendE)
                    pendE = (a, ab, tok, i)
                stageE(pendE)
                for cg in range(2):
                    wt, wtb = self.wtile("gla_w_out", 0, 8, cg * 512, 512)
                    for i in range(4):
                        ti = 4 * g + i
                        px, pxb = self.ps(1)
                        for kc in range(8):
                            self.mm(px, AT[:, kc, i * 128:(i + 1) * 128], wt[:, kc, :], [ATb[i], wtb], pxb,
                                    start=(kc == 0), stop=(kc == 7))
                        xs = self.X[:, ti, cg * 512:(cg + 1) * 512]
                        self.tt("dve", xs, xs, px, ALU.add, pxb + [self.Xb[ti]], [self.Xb[ti]])
                        if cg == 1:
                            pend_final.append(ti)
            for ti in pend_final:
                self.x_final(ti)
        self.barrier()

    def ffn(self, l):
        S = self.S
        G = 1024
        wn_up, wn_dn = f"ffn_w_up{l}", f"ffn_w_down{l}"
        gcol = CP_G[f"g_ffn{l}"]
        self.stats_begin()
        with ExitStack() as sc:
            self.extra_slots(sc, 2)
            HT = self.sb(sc, "HTf", [128, 8, G], BF16); HTb = Buf("HTf")
            AT = self.sb(sc, "ATf", [128, NFC, G], BF16); ATb = Buf("ATf")
            uur = [self.sb(sc, f"uu{i}", [128, G + 2], F32) for i in range(3)]; uurb = [Buf() for _ in range(3)]
            uurh = [Buf() for _ in range(3)]
            ccr = [self.sb(sc, f"cc{i}", [128, G], F32) for i in range(4)]; ccrb = [Buf() for _ in range(4)]
            rpos = 0

            def finish_pair(jp, resp):
                (cA, cAb), (cB, cBb) = resp
                self.act(cA[:], cA[:], AF.Silu, [cAb], [cAb])
                self.tt("dve", AT[:, jp, :], cA[:], cB[:], ALU.mult, [cAb, cBb], [ATb])
            hal = self.sb(sc, "hal", [128, 44, 2], F32); halb = Buf("hal")
            self.memset("pool", hal[:], 0.0, [halb])
            psrc = self.dram["p"][l, self.cur_seq, 0:512, :].rearrange("(i t) c -> t i c", t=128)
            S.dma("pool", "pldF", self.pbF[:], psrc, writes=[self.pbFb])
            ngen = None
            for g in range(T // G):
                if ngen is None:
                    self.phase_rstd(8 * g, 8, src=(self.rd, self.rdb))
                    self.norm_group(8 * g, 8, gcol, HT, HTb)
                else:
                    for _ in ngen:
                        pass
                ngen = None
                pend = None
                for j in range(NFC):
                    slot_i = self.wpos % len(self.wslots)
                    self.wpos += 1
                    wt, wtb = self.wslots[slot_i], self.wslotb[slot_i]
                    wsrc = self.wd[wn_up].rearrange("(c p) n -> p c n", p=128)
                    aft = self.after if slot_i >= 2 else ()
                    S.dma("sp", f"wr{slot_i}", wt[:, :, 0:128], wsrc[:, :, j * 128:(j + 1) * 128],
                          reads=[self.wdb[wn_up]], writes=[wtb], after=aft)
                    S.dma("sp", f"wr{slot_i}", wt[:, :, 128:256],
                          wsrc[:, :, DFF + j * 128:DFF + (j + 1) * 128], reads=[self.wdb[wn_up]], writes=[wtb],
                          after=aft)
                    res = []
                    for half, jj in enumerate([j, NFC + j]):
                        uu, uub, cc, ccb = uur[rpos % 3], uurb[rpos % 3], ccr[rpos % 4], ccrb[rpos % 4]
                        uuh = uurh[rpos % 3]
                        rpos += 1
                        self.cp("pool", uu[:, 0:2], hal[:, jj, :], [halb], [uuh])
                        for tg in range(G // 512):
                            pu, pub = self.ps(1)
                            for kc in range(8):
                                self.mm(pu, wt[:, kc, half * 128:(half + 1) * 128],
                                        HT[:, kc, tg * 512:(tg + 1) * 512], [wtb, HTb], pub,
                                        start=(kc == 0), stop=(kc == 7))
                            self.cp("act", uu[:, 2 + tg * 512:2 + (tg + 1) * 512], pu, pub, [uub])
                        self.cp("pool", hal[:, jj, :], uu[:, G:G + 2], [uub], [halb])
                        w0 = self.cpp[:, CP_WDW + (l * 3 + 0) * 44 + jj:CP_WDW + (l * 3 + 0) * 44 + jj + 1]
                        w1 = self.cpp[:, CP_WDW + (l * 3 + 1) * 44 + jj:CP_WDW + (l * 3 + 1) * 44 + jj + 1]
                        w2 = self.cpp[:, CP_WDW + (l * 3 + 2) * 44 + jj:CP_WDW + (l * 3 + 2) * 44 + jj + 1]
                        bb = self.cpp[:, CP_BDW + l * 44 + jj:CP_BDW + l * 44 + jj + 1]
                        self.act(cc[:], uu[:, 2:G + 2], AF.Identity, [uub, self.cbb], [ccb], bias=bb, scale=w2)
                        self.stt("dve", cc[:], uu[:, 1:G + 1], w1, cc[:], ALU.mult, ALU.add,
                                 [uub, uuh, ccb, self.cbb], [ccb])
                        self.stt("dve", cc[:], uu[:, 0:G], w0, cc[:], ALU.mult, ALU.add,
                                 [uub, uuh, ccb, self.cbb], [ccb])
                        res.append((cc, ccb))
                        if half == 0 and pend is not None:
                            finish_pair(*pend)
                    pend = (j, res)
                finish_pair(*pend)
                if g + 1 < T // G:
                    self.phase_rstd(8 * (g + 1), 8, src=(self.rd, self.rdb))
                    ngen = self.norm_gen(8 * (g + 1), 8, gcol, HT, HTb)
                for cg in range(2):
                    wts = []
                    for part, (k0, nk) in enumerate([(0, 8), (8, 8), (16, 6)]):
                        wts.append(self.wtile(wn_dn, k0, nk, cg * 512, 512) + (k0, nk))
                    for i in range(G // 128):
                        if ngen is not None and cg == 0:
                            next(ngen, None)
                        ti = 8 * g + i
                        px, pxb = self.ps(1)
                        for (wt, wtb, k0, nk) in wts:
                            for kk in range(nk):
                                j = k0 + kk
                                self.mm(px, AT[:, j, i * 128:(i + 1) * 128], wt[:, kk, :], [ATb, wtb], pxb,
                                        start=(j == 0), stop=(j == NFC - 1))
                        xs = self.X[:, ti, cg * 512:(cg + 1) * 512]
                        self.tt("dve", xs, xs, px, ALU.add, pxb + [self.Xb[ti]], [self.Xb[ti]])
                        if cg == 1:
                            self.x_final(ti)
        self.barrier()

    def ple(self, l, seq):
        S = self.S
        gcol = CP_G[f"g_ple{l}"]
        self.stats_begin()
        NG, NTG = 2, 8
        with ExitStack() as sc:
            self.extra_slots(sc, 2)
            HT = self.sb(sc, "HTp", [128, 8, NTG * 128], BF16); HTb = [Buf(f"HTp{i}") for i in range(NTG)]
            pb16s = [self.sb(sc, f"pb16_{i}", [128, NTG, 256], BF16) for i in range(NG)]
            pb16bs = [Buf(f"pb16_{i}") for i in range(NG)]
            for g in range(NG):
                i0_ = 4 if g == 0 else 0
                src = self.dram["p"][l, seq, (g * NTG + i0_) * 128:(g + 1) * NTG * 128, :].rearrange(
                    "(i t) c -> t i c", t=128)
                S.dma("pool", f"pld{g}", pb16s[g][:, i0_:NTG, :], src, writes=[pb16bs[g]], after=self.after)
            pT = self.sb(sc, "pT", [128, 2, NTG * 128], BF16); pTb = [Buf(f"pT{i}") for i in range(NTG)]
            sgt = [self.sb(sc, f"sgp{i}", [128, 512], F32) for i in range(2)]; sgtb = [Buf(), Buf()]
            tmp = [self.sb(sc, f"tmpp{i}", [128, 512], F32) for i in range(2)]; tmpb = [Buf(), Buf()]
            for g in range(NG):
                pb16, pb16b = pb16s[g], pb16bs[g]
                self.phase_rstd(NTG * g, NTG, src=(self.rd, self.rdb))
                ngen = self.norm_gen(NTG * g, NTG, gcol, HT, HTb)
                next(ngen, None)

                def ptrans(i):
                    pt, ptb_ = self.ps(1)
                    ptb = pt.bitcast(BF16)
                    pin, pinb = (self.pbF, self.pbFb) if (g == 0 and i < 4) else (pb16, pb16b)
                    for c in range(2):
                        self.tr(ptb[:, c * 128:(c + 1) * 128], pin[:, i, c * 128:(c + 1) * 128], [pinb], ptb_)
                    self.cp("dve", pT[:, :, i * 128:(i + 1) * 128],
                            ptb[:, 0:256].rearrange("p (c t) -> p c t", t=128), ptb_, [pTb[i]])

                ptrans(0)
                for cg in range(2):
                    wg, wgb = self.wtile(f"ple_w_gate{l}", 0, 8, cg * 512, 512)
                    wp, wpb = self.wtile(f"ple_w_proj{l}", 0, 2, cg * 512, 512)
                    for i in range(NTG):
                        if cg == 0:
                            next(ngen, None)
                            if i + 1 < NTG:
                                ptrans(i + 1)
                        ti = NTG * g + i
                        tok = slice(i * 128, (i + 1) * 128)
                        k = (cg * NTG + i) % 2
                        pg, pgb = self.ps(1)
                        for kc in range(8):
                            self.mm(pg, HT[:, kc, tok], wg[:, kc, :], [HTb[i], wgb], pgb, start=(kc == 0), stop=False)
                        self.mm(pg, self.ones[0:1, :], self.bgate[0:1, l, cg * 512:(cg + 1) * 512], [self.cbb], pgb,
                                start=False, stop=True)
                        self.act(sgt[k][:], pg, AF.Sigmoid, pgb, [sgtb[k]])
                        pp, ppb = self.ps(1)
                        for c in range(2):
                            self.mm(pp, pT[:, c, tok], wp[:, c, :], [pTb[i], wpb], ppb, start=(c == 0), stop=(c == 1))
                        self.tt("dve", tmp[k][:], pp, sgt[k][:], ALU.mult, ppb + [sgtb[k]], [tmpb[k]])
                        xs = self.X[:, ti, cg * 512:(cg + 1) * 512]
                        self.tt("dve", xs, xs, tmp[k][:], ALU.add, [tmpb[k], self.Xb[ti]], [self.Xb[ti]])
                        if cg == 1:
                            self.x_final(ti)
        self.barrier()

    def headnorm(self, ps_ap, psb, dst, gcol, qscale, tmp, tmpb, hsq, hsqb, hrs, hrsb):
        self.act(tmp[:], ps_ap, AF.Square, psb, [tmpb])
        self.S.op("dve", lambda e: e.tensor_reduce(hsq[:], tmp[:].rearrange("p (h d) -> p h d", d=64), AX.X,
                                                   ALU.add), [tmpb], [hsqb])
        self.act(hsq[:], hsq[:], AF.Sqrt, [hsqb], [hsqb], bias=EPS, scale=1.0 / 64)
        self.recip(hrs[:], hsq[:], [hsqb], [hrsb])
        p3 = ps_ap.rearrange("p (h d) -> p h d", d=64)
        t3 = tmp[:].rearrange("p (h d) -> p h d", d=64)
        self.stt("dve", t3, p3, qscale, hrs[:].unsqueeze(2).broadcast_to([128, 8, 64]), ALU.mult, ALU.mult,
                 psb + [hrsb, tmpb], [tmpb])
        gbc = self.roww[:, gcol:gcol + 64].unsqueeze(1).broadcast_to([128, 8, 64])
        return t3, gbc

    def fox(self):
        S = self.S
        HD = 70
        if getattr(self, "fresh_x", False):
            self.fresh_x = False
            for i in range(NT):
                self.x_final(i)
        self.stats_begin()
        with ExitStack() as sc:
            self.extra_slots(sc, 1)
            ka = self.sb(sc, "ka", [128, NT, 16, HD], BF16); kab = Buf("ka")
            va = self.sb(sc, "va", [128, NT, 16, 65], BF16); vab = Buf("va")
            wf = self.sb(sc, "wf", [128, 8, 16], BF16); wfb = Buf("wf")
            S.dma("sp", "wfl", wf[:], self.wd["kv_w_in"].rearrange("(c p) n -> p c n", p=128)[:, :, 2048:2064],
                  reads=[self.wdb["kv_w_in"]], writes=[wfb], after=self.after)
            HT = self.sb(sc, "HTx", [128, 8, 512], BF16); HTb = [Buf(f"HTx{i}") for i in range(4)]
            qa = self.sb(sc, "qa", [128, 4, 16, HD], BF16); qab = [Buf(f"qa{h}") for h in range(16)]
            csp = self.sb(sc, "csp", [128, 3, NT, 16], BF16); cspb = Buf("csp")
            tmp = [self.sb(sc, f"hn{i}", [128, 512], F32) for i in range(2)]; tmpb = [Buf(), Buf()]
            hsq2 = [self.sb(sc, f"hsq{i}", [128, 8], F32) for i in range(2)]; hsq2b = [Buf(), Buf()]
            hrs2 = [self.sb(sc, f"hrs{i}", [128, 8], F32) for i in range(2)]; hrs2b = [Buf(), Buf()]
            kvsc = ExitStack()
            spa = self.sb(kvsc, "spa", [128, NT, 16], F32); spab = Buf("spa")
            call = self.sb(kvsc, "call", [128, NT, 16], F32); callb = Buf("call")
            r1, r1b = spa, spab
            fsb = self.sb(kvsc, "fsb", [128, 16], F32); fsbb = Buf("fsb")
            self.memset("pool", va[:, :, :, 64:65], 1.0, [vab])
            self.memset("pool", ka[:, :, :, 67:70], 1.0, [kab])
            for h in range(16):
                self.memset("pool", qa[:, :, h, 64:67], 1.0, [qab[h]])
            for g in range(4):
                self.phase_rstd(4 * g, 4, src=(self.rd, self.rdb))
                ngen = self.norm_gen(4 * g, 4, CP_G["kv_g"], HT, HTb)
                next(ngen, None)
                for which in range(2):
                    for cg in range(2):
                        wt, wtb = self.wtile("kv_w_in", 0, 8, which * 1024 + cg * 512, 512)
                        for i in range(4):
                            next(ngen, None)
                            ti = 4 * g + i
                            tok = slice(i * 128, (i + 1) * 128)
                            pk, pkb = self.ps(1)
                            for kc in range(8):
                                self.mm(pk, HT[:, kc, tok], wt[:, kc, :], [wtb, HTb[i]], pkb,
                                        start=(kc == 0), stop=(kc == 7))
                            if which == 0:
                                k_ = (cg * 4 + i) % 2
                                t3, gbc = self.headnorm(pk, pkb, None, RW_GK, 1.0, tmp[k_], tmpb[k_], hsq2[k_],
                                                        hsq2b[k_], hrs2[k_], hrs2b[k_])
                                self.tt("dve", ka[:, ti, cg * 8:(cg + 1) * 8, 0:64], t3, gbc, ALU.mult,
                                        [tmpb[k_], self.cbb], [kab])
                            else:
                                self.cp("act", va[:, ti, cg * 8:(cg + 1) * 8, 0:64],
                                        pk.rearrange("p (h d) -> p h d", d=64), pkb, [vab])
                for i in range(4):
                    ti = 4 * g + i
                    tok = slice(i * 128, (i + 1) * 128)
                    pf, pfb = self.ps(1)
                    for kc in range(8):
                        self.mm(pf[:, 0:16], HT[:, kc, tok], wf[:, kc, :], [wfb, HTb[i]], pfb,
                                start=(kc == 0), stop=(kc == 7))
                    self.tt("dve", fsb[:], pf[:, 0:16], self.roww[:, RW_BF:RW_BF + 16], ALU.add,
                            pfb + [self.cbb], [fsbb])
                    self.act(fsb[:], fsb[:], AF.Exp, [fsbb], [fsbb], scale=-1.0)
                    self.act(spa[:, ti, :], fsb[:], AF.Ln, [fsbb], [spab], bias=1.0)
            for ti in range(NT):
                pc, pcb = self.ps(1)
                self.mm(pc[:, 0:16], self.trineg, spa[:, ti, :], [spab, self.cbb], pcb)
                if ti > 0:
                    self.mm(pc[:, 16:32], self.onesneg, spa[:, ti - 1, :], [spab, self.cbb], pcb)
                    if ti == 1:
                        self.cp("dve", fsb[:], pc[:, 16:32], pcb, [fsbb])
                    else:
                        self.tt("dve", fsb[:], fsb[:], pc[:, 16:32], ALU.add, pcb + [fsbb], [fsbb])
                    self.tt("dve", call[:, ti, :], pc[:, 0:16], fsb[:], ALU.add, pcb + [fsbb], [callb])
                else:
                    self.cp("dve", call[:, ti, :], pc[:, 0:16], pcb, [callb])
            self.cp("dve", csp[:, 0], call[:], [callb], [cspb])
            self.tt("dve", r1[:], call[:], csp[:, 0], ALU.subtract, [callb, cspb], [r1b])
            self.cp("dve", csp[:, 1], r1[:], [r1b], [cspb])
            self.tt("dve", r1[:], r1[:], csp[:, 1], ALU.subtract, [r1b, cspb], [r1b])
            self.cp("dve", csp[:, 2], r1[:], [r1b], [cspb])
            for k in range(3):
                self.ts("dve", ka[:, :, :, 64 + k], csp[:, k], -1.0, None, ALU.mult, None, [cspb], [kab])
            self.after = self.S.barrier()
            kvsc.close()
            kT = [self.sb(sc, f"kT{i}", [HD, T], BF16) for i in range(2)]; kTb = [Buf(), Buf()]
            qT = [self.sb(sc, f"qT{i}", [HD, 512], BF16) for i in range(2)]; qTb = [Buf(), Buf()]
            ptsall = self.sb(sc, "ptsall", [128, 3, 512], BF16)
            pts = [ptsall[:, i, :] for i in range(3)]; ptsb = [Buf() for _ in range(3)]
            rden = self.sb(sc, "rden", [128, 4], F32); rdenb = Buf("rden")
            a2v = ptsall[:, 0:2, :].rearrange("p a b -> p (a b)")
            a2bs = [ptsb[0], ptsb[1]]
            self.psn = 6
            self.pspos = 0
            pob = [self.psb[6], self.psb[7]]
            pot = [self.pst[3][:, 0:512], self.pst[3][:, 512:1024]]
            hcount = 0
            for g in range(4):
                ngen = self.norm_gen(4 * g, 4, CP_G["g_mix1"], HT, HTb)
                next(ngen, None)
                for cg in range(2):
                    wt, wtb = self.wtile("fox_w_in", 0, 8, cg * 512, 512)
                    for i in range(4):
                        next(ngen, None)
                        tok = slice(i * 128, (i + 1) * 128)
                        pq, pqb = self.ps(1)
                        for kc in range(8):
                            self.mm(pq, HT[:, kc, tok], wt[:, kc, :], [wtb, HTb[i]], pqb,
                                    start=(kc == 0), stop=(kc == 7))
                        k_ = (cg * 4 + i) % 2
                        t3, gbc = self.headnorm(pq, pqb, None, RW_GQ, 0.125, tmp[k_], tmpb[k_], hsq2[k_],
                                                hsq2b[k_], hrs2[k_], hrs2b[k_])
                        self.tt("dve", qa[:, i, cg * 8:(cg + 1) * 8, 0:64], t3, gbc, ALU.mult,
                                [tmpb[k_], self.cbb], qab[cg * 8:(cg + 1) * 8])
                for k in range(3):
                    self.cp("pool", qa[:, :, :, 67 + k], csp[:, k, 4 * g:4 * g + 4, :], [cspb], qab)
                nk = 4 * g + 4
                def prep(h, hb):
                    pt, ptb_ = self.ps(1)
                    ptb = pt.bitcast(BF16)
                    for i in range(4):
                        self.tr(ptb[0:HD, i * 128:(i + 1) * 128], qa[:, i, h, :], [qab[h]], ptb_)
                    self.cp("dve", qT[hb][:, :], ptb[0:HD, 0:512], ptb_, [qTb[hb]])
                    for j0 in range(0, nk, 4):
                        pt2, pt2b = self.ps(1)
                        pt2v = pt2.bitcast(BF16)
                        for jj in range(4):
                            self.tr(pt2v[0:HD, jj * 128:(jj + 1) * 128], ka[:, j0 + jj, h, :], [kab], pt2b)
                        self.cp("dve", kT[hb][:, j0 * 128:(j0 + 4) * 128], pt2v[0:HD, 0:512], pt2b, [kTb[hb]])

                def qk_exp(h, hb, j):
                    ks = slice(j * 128, (j + 1) * 128)
                    pS, pSb = self.ps(1)
                    i0 = max(0, j - 4 * g)
                    if j >= 4 * g:
                        dg = slice(i0 * 128, (i0 + 1) * 128)
                        self.mm(pS[:, dg], self.ident, self.maskneg, [self.cbb], pSb, start=True, stop=False)
                        self.mm(pS[:, dg], kT[hb][:, ks], qT[hb][:, dg], [kTb[hb], qTb[hb]], pSb,
                                start=False, stop=True)
                        if i0 < 3:
                            rest = slice((i0 + 1) * 128, 512)
                            self.mm(pS[:, rest], kT[hb][:, ks], qT[hb][:, rest], [kTb[hb], qTb[hb]], pSb)
                    else:
                        self.mm(pS, kT[hb][:, ks], qT[hb][:, :], [kTb[hb], qTb[hb]], pSb)
                    pp, ppb = pts[j % 3], ptsb[j % 3]
                    self.act(pp[:, i0 * 128:512], pS[:, i0 * 128:512], AF.Exp, pSb, [ppb])
                    return (j, i0, pp, ppb)

                def pv(h, hb, blk):
                    j, i0, pp, ppb = blk
                    po, pobuf = pot[hb], [pob[hb]]
                    for i in range(i0, 4):
                        self.mm(po[:, i * 128:i * 128 + 65], pp[:, i * 128:(i + 1) * 128], va[:, j, h, :],
                                [ppb, vab], pobuf, start=(j == 0 and i == 0), stop=(j == 4 * g + i))

                prep(0, hcount % 2)
                for h in range(16):
                    hb = hcount % 2
                    hcount += 1
                    if h + 1 < 16:
                        prep(h + 1, hcount % 2)
                    pendq = []
                    for j in range(nk):
                        pendq.append(qk_exp(h, hb, j))
                        if len(pendq) > 2:
                            pv(h, hb, pendq.pop(0))
                    for blk in pendq:
                        pv(h, hb, blk)
                    po = pot[hb]
                    pobuf = [pob[hb]]
                    po3 = po.rearrange("p (i c) -> p i c", c=128)
                    self.recip(rden[:].unsqueeze(2), po3[:, :, 64:65], pobuf, [rdenb])
                    self.tt("dve", qa[:, :, h, 0:64], po3[:, :, 0:64],
                            rden[:].unsqueeze(2).broadcast_to([128, 4, 64]), ALU.mult, pobuf + [rdenb], [qab[h]])
                wog = [self.wtile("fox_w_in", 0, 8, 1024 + cg * 512, 512) for cg in range(2)]
                for i in range(4):
                    tok = slice(i * 128, (i + 1) * 128)
                    for cg in range(2):
                        wt, wtb = wog[cg]
                        pg, pgb = self.ps(1)
                        for kc in range(8):
                            self.mm(pg, HT[:, kc, tok], wt[:, kc, :], [wtb, HTb[i]], pgb,
                                    start=(kc == 0), stop=(kc == 7))
                        k_ = (cg * 4 + i) % 2
                        self.act(tmp[k_][:], pg, AF.Sigmoid, pgb, [tmpb[k_]])
                        av = qa[:, i, cg * 8:(cg + 1) * 8, 0:64]
                        self.tt("dve", a2v[:, cg * 512:(cg + 1) * 512].rearrange("p (h d) -> p h d", d=64), av,
                                tmp[k_][:].rearrange("p (h d) -> p h d", d=64), ALU.mult,
                                [tmpb[k_]] + qab[cg * 8:(cg + 1) * 8], a2bs)
                    pt, ptb_ = self.ps(1)
                    ptb = pt.bitcast(BF16)
                    for kc in range(8):
                        self.tr(ptb[:, kc * 128:(kc + 1) * 128], a2v[:, kc * 128:(kc + 1) * 128], a2bs, ptb_)
                    self.cp("dve", HT[:, :, tok], ptb.rearrange("p (c t) -> p c t", t=128), ptb_, [HTb[i]])
                for cg in range(2):
                    wt, wtb = self.wtile("fox_w_out", 0, 8, cg * 512, 512)
                    for i in range(4):
                        ti = 4 * g + i
                        px, pxb = self.ps(1)
                        for kc in range(8):
                            self.mm(px, HT[:, kc, i * 128:(i + 1) * 128], wt[:, kc, :], [HTb[i], wtb], pxb,
                                    start=(kc == 0), stop=(kc == 7))
                        xs = self.X[:, ti, cg * 512:(cg + 1) * 512]
                        self.tt("dve", xs, xs, px, ALU.add, pxb + [self.Xb[ti]], [self.Xb[ti]])
                        if cg == 1:
                            self.x_final(ti)
            self.psn = 8
        self.barrier()

    def store_raw(self, seq, out):
        for i in range(NT):
            self.S.dma("sp", f"xs{i}", out[seq, i * 128:(i + 1) * 128, :], self.X[:, i, :], reads=[self.Xb[i]])

    def final_norm_store(self, seq, out, nseq=None):
        S = self.S
        ss, sq, rs = self.nss, self.nsq, self.nrs
        with ExitStack() as sc:
            NOB = 6
            ob = [self.sb(sc, f"ob{i}", [128, 1024], F32) for i in range(NOB)]; obb = [Buf() for _ in range(NOB)]
            gfin = self.sb(sc, "gfin", [128, 1024], F32); gfinb = Buf("gfin")
            S.dma("sp", "cst", gfin[:], self.dram["gfin"].partition_broadcast(128), writes=[gfinb], after=self.after)
            self.phase_rstd()
            for i in range(NT):
                o, obf = ob[i % NOB], obb[i % NOB]
                self.stt("dve", o[:], self.X[:, i, :], rs[:, i:i + 1], gfin[:],
                         ALU.mult, ALU.mult, [self.Xb[i], self.nrsb, gfinb], [obf])
                S.dma("pool", f"ost{i % NOB}", out[seq, i * 128:(i + 1) * 128, :], o[:], reads=[obf], after=self.after)
                if nseq is not None:
                    S.dma("pool", f"xl{i}", self.X[:, i, :], self.dram["x"][nseq, i * 128:(i + 1) * 128, :],
                          writes=[self.Xb[i]])
        self.barrier()

    def finish(self):
        S = self.S
        for key in list(S.sems.keys()):
            if key.startswith("ost") or key.startswith("xs"):
                S._need("sp", (key, S.cnt[key]), False)
        for e in ("pe", "act", "dve", "pool"):
            if S.cnt[e] > 0:
                S._need("sp", (e, S.cnt[e]), False)
        S.emit()


def build(nseq, layers=(0, 1), final=True):
    nc = bass.Bass("TRN2", target_bir_lowering=False)
    dram = {}
    dram["x"] = nc.dram_tensor("x", [nseq, T, D], F32, kind="ExternalInput").ap()
    dram["p"] = nc.dram_tensor("p", [2, nseq, T, 256], F32, kind="ExternalInput").ap()
    for name, K, N in WNAMES:
        dram[name] = nc.dram_tensor(name, [K, N], F32, kind="ExternalInput").ap()
    dram["cpp"] = nc.dram_tensor("cpp", [128, NCP], F32, kind="ExternalInput").ap()
    dram["roww"] = nc.dram_tensor("roww", [1, NRW], F32, kind="ExternalInput").ap()
    dram["gfin"] = nc.dram_tensor("gfin", [1, 1024], F32, kind="ExternalInput").ap()
    dram["gonrow"] = nc.dram_tensor("gonrow", [1, 1024], F32, kind="ExternalInput").ap()
    dram["cf32"] = nc.dram_tensor("cf32", [128, 640], F32, kind="ExternalInput").ap()
    dram["cbf"] = nc.dram_tensor("cbf", [128, 256], BF16, kind="ExternalInput").ap()
    out = nc.dram_tensor("out", [nseq, T, D], F32, kind="ExternalOutput").ap()
    with ExitStack() as es:
        kb = KB(nc, es, nseq)
        kb.setup(dram)
        for s in range(nseq):
            kb.load_x(s)
            if 0 in layers:
                kb.gla()
                kb.ffn(0)
                kb.ple(0, s)
            if 1 in layers:
                kb.fox()
                kb.ffn(1)
                kb.ple(1, s)
            if final:
                nseq_ = s + 1 if s + 1 < nseq else None
                kb.final_norm_store(s, out, nseq_)
                kb.preloaded = nseq_ is not None
            else:
                kb.store_raw(s, out)
        kb.finish()
    return nc


def run_layers(x, p, shared, layers, final, ncores=NCORES):
    B = x.shape[0]
    nseq = B // ncores
    nc = build(nseq, layers, final)
    in_maps = []
    for c in range(ncores):
        m = dict(shared)
        m["x"] = np.ascontiguousarray(x[c * nseq:(c + 1) * nseq])
        m["p"] = np.ascontiguousarray(p[:, c * nseq:(c + 1) * nseq])
        in_maps.append(m)
    res = run_bass_kernel_spmd(nc, in_maps, core_ids=list(range(ncores)))
    return np.concatenate([np.asarray(r["out"]) for r in res.results], axis=0)


MODE = "fused"


def kernel(**inputs):
    inp = {k: np.asarray(v) for k, v in inputs.items()}
    shared = _prep_shared(inp)
    x = np.ascontiguousarray(inp["x"], dtype=np.float32)
    p = np.ascontiguousarray(inp["p"], dtype=np.float32)
    if MODE == "fused":
        out = run_layers(x, p, shared, (0, 1), True)
    else:
        x1 = run_layers(x, p, shared, (0,), False)
        out = run_layers(x1, p, shared, (1,), True)
    return out.astype(np.float32)
```

```python
from contextlib import ExitStack
import numpy as np
import ml_dtypes
import concourse.bass as bass
import concourse.mybir as mybir
from concourse.bass_utils import run_bass_kernel_spmd

F32 = mybir.dt.float32
BF16 = mybir.dt.bfloat16
AF = mybir.ActivationFunctionType
ALU = mybir.AluOpType
AX = mybir.AxisListType

D = 1024
T = 2048
NT = 16
DFF = 2816
NFC = 22
EPS = 1e-6
NCORES = 8


class Buf:
    __slots__ = ("name", "w", "r")

    def __init__(self, name=""):
        self.name = name
        self.w = None
        self.r = {}


class Sched:
    ENG = ("pe", "act", "dve", "pool", "sp")

    def __init__(self, nc, es):
        self.nc = nc
        self.es = es
        self.streams = {k: [] for k in self.ENG}
        self.sems = {}
        self.cnt = {}
        self.waited = {k: {} for k in self.ENG}
        for k in self.ENG:
            self._mksem(k)

    def _mksem(self, key):
        self.sems[key] = self.es.enter_context(self.nc.semaphore("s_" + key))
        self.cnt[key] = 0

    def _need(self, eng, ev, same_ok):
        if ev is None:
            return
        key, val = ev
        if same_ok and key == eng:
            return
        if self.waited[eng].get(key, 0) >= val:
            return
        self.waited[eng][key] = val
        self.streams[eng].append(("w", key, val))

    def _deps(self, eng, reads, writes, is_dma):
        for b in reads:
            self._need(eng, b.w, False)
        for b in writes:
            self._need(eng, b.w, not is_dma)
            for k, v in b.r.items():
                self._need(eng, (k, v), not is_dma)

    def _mark(self, ev, reads, writes):
        for b in writes:
            b.w = ev
            b.r = {}
        for b in reads:
            if b.r.get(ev[0], 0) < ev[1]:
                b.r[ev[0]] = ev[1]

    def op(self, eng, fn, reads=(), writes=()):
        self._deps(eng, reads, writes, False)
        self.cnt[eng] += 1
        ev = (eng, self.cnt[eng])
        self.streams[eng].append(("o", fn, eng, 1))
        self._mark(ev, reads, writes)
        return ev

    def dma(self, eng, semkey, out, in_, reads=(), writes=(), after=()):
        if semkey not in self.sems:
            self._mksem(semkey)
        for ev in after:
            self._need(eng, ev, False)
        self._deps(eng, reads, writes, True)
        self.cnt[semkey] += 16
        ev = (semkey, self.cnt[semkey])
        self.streams[eng].append(("o", lambda e: e.dma_start(out=out, in_=in_), semkey, 16))
        self._mark(ev, reads, writes)
        return ev

    def barrier(self):
        evs = [(k, self.cnt[k]) for k in ("pe", "act", "dve", "pool") if self.cnt[k] > 0]
        for e in ("pe", "act", "dve", "pool"):
            for ev in evs:
                if ev[0] != e:
                    self._need(e, ev, False)
        return evs

    def emit(self):
        nc = self.nc
        with nc.Block() as block:
            def mk(key):
                def body(e):
                    for it in self.streams[key]:
                        if it[0] == "w":
                            e.wait_ge(self.sems[it[1]], it[2])
                        else:
                            it[1](e).then_inc(self.sems[it[2]], it[3])
                return body
            block.tensor(mk("pe"))
            block.scalar(mk("act"))
            block.vector(mk("dve"))
            block.gpsimd(mk("pool"))
            block.sync(mk("sp"))


WNAMES = [("gla_w_in", 1024, 3088), ("gla_w_out", 1024, 1024), ("kv_w_in", 1024, 2064),
          ("fox_w_in", 1024, 2048), ("fox_w_out", 1024, 1024),
          ("ffn_w_up0", 1024, 5632), ("ffn_w_up1", 1024, 5632),
          ("ffn_w_down0", 2816, 1024), ("ffn_w_down1", 2816, 1024),
          ("ple_w_gate0", 1024, 1024), ("ple_w_gate1", 1024, 1024),
          ("ple_w_proj0", 256, 1024), ("ple_w_proj1", 256, 1024),
          ("gk2aug", 17, 512), ("bgate", 2, 1024)]

CP_G = {"g_mix0": 0, "g_ffn0": 8, "g_ple0": 16, "kv_g": 24, "g_mix1": 32, "g_ffn1": 40, "g_ple1": 48,
        "gon": 56}
CP_WDW = 64
CP_BDW = 64 + 264
NCP = 64 + 264 + 88
RW_GFIN = 0
RW_GQ = 0
RW_GK = 64
RW_BF = 128
NRW = 144


def _host_consts():
    s = np.arange(128)[:, None]
    c = np.arange(128)[None, :]
    le = (s <= c).astype(np.float32)
    gt = (s > c).astype(np.float32)
    cf = np.concatenate([le, -le / 16.0, -gt / 16.0, -le, -np.ones((128, 128), np.float32)], axis=1)
    cb = np.concatenate([np.eye(128, dtype=np.float32), -30000.0 * gt], axis=1).astype(ml_dtypes.bfloat16)
    return np.ascontiguousarray(cf, np.float32), np.ascontiguousarray(cb)


def _fm(v):
    v = np.asarray(v, np.float32)
    return np.ascontiguousarray(v.reshape(-1, 128).T)


def _prep_shared(inp):
    sh = {}
    sh["gla_w_in"] = inp["gla_w_in"][0]
    sh["gla_w_out"] = inp["gla_w_out"][0]
    sh["kv_w_in"] = inp["kv_w_in"]
    sh["fox_w_in"] = inp["fox_w_in"][0]
    sh["fox_w_out"] = inp["fox_w_out"][0]
    for l in range(2):
        sh[f"ffn_w_up{l}"] = inp["ffn_w_up"][l]
        sh[f"ffn_w_down{l}"] = inp["ffn_w_down"][l]
        sh[f"ple_w_gate{l}"] = inp["ple_w_gate"][l]
        sh[f"ple_w_proj{l}"] = inp["ple_w_proj"][l]
    sh["gk2aug"] = np.concatenate([inp["gla_w_gk2"][0], inp["gla_b_gk2"][0][None, :]], axis=0)
    sh["bgate"] = inp["ple_b_gate"]
    cp = np.zeros((128, NCP), np.float32)
    cp[:, 0:8] = _fm(inp["g_mix"][0]); cp[:, 8:16] = _fm(inp["g_ffn"][0]); cp[:, 16:24] = _fm(inp["g_ple"][0])
    cp[:, 24:32] = _fm(inp["kv_g_norm"]); cp[:, 32:40] = _fm(inp["g_mix"][1])
    cp[:, 40:48] = _fm(inp["g_ffn"][1]); cp[:, 48:56] = _fm(inp["g_ple"][1])
    cp[:, 56:64] = _fm(np.tile(inp["gla_g_onorm"][0], 4))
    for l in range(2):
        for tap in range(3):
            o = CP_WDW + (l * 3 + tap) * 44
            cp[:, o:o + 44] = _fm(inp["ffn_w_dw"][l, tap])
        o = CP_BDW + l * 44
        cp[:, o:o + 44] = _fm(inp["ffn_b_dw"][l])
    sh["cpp"] = cp
    rw = np.zeros((1, NRW), np.float32)
    rw[0, RW_GQ:RW_GQ + 64] = inp["fox_g_qnorm"][0]
    rw[0, RW_GK:RW_GK + 64] = inp["kv_g_knorm"]
    rw[0, RW_BF:RW_BF + 16] = inp["kv_b_f"]
    sh["roww"] = rw
    sh["gfin"] = np.asarray(inp["g_final"], np.float32).reshape(1, 1024)
    sh["gonrow"] = np.tile(np.asarray(inp["gla_g_onorm"][0], np.float32), 4).reshape(1, 1024)
    cf, cb = _host_consts()
    sh["cf32"] = cf
    sh["cbf"] = cb
    return {k: np.ascontiguousarray(np.asarray(v)) for k, v in sh.items()}


class KB:
    def __init__(self, nc, es, nseq):
        self.nc = nc
        self.es = es
        self.nseq = nseq
        self.S = Sched(nc, es)
        self.uid = 0
        self.after = []
        self.pst = [es.enter_context(nc.psum_tensor(f"ps{i}", [128, 1024], F32)) for i in range(4)]
        self.psb = [Buf(f"psb{i}") for i in range(8)]
        self.pspos = 0
        self.psn = 8

    def sb(self, scope, name, shape, dt):
        self.uid += 1
        return scope.enter_context(self.nc.sbuf_tensor(f"{name}_{self.uid}", shape, dt))

    def ps(self, n=1):
        if n == 2 and self.pspos % 2:
            self.pspos += 1
        b = self.pspos % self.psn
        if b + n > self.psn:
            self.pspos += self.psn - b
            b = 0
        self.pspos += n
        t = self.pst[b // 2]
        off = (b % 2) * 512
        return t[:, off:off + 512 * n], self.psb[b:b + n]

    def mm(self, out, lhsT, rhs, r, w, start=True, stop=True):
        self.S.op("pe", lambda e: e.matmul(out, lhsT, rhs, start=start, stop=stop), r, w)

    def tr(self, out, in_, r, w):
        kp = in_.shape[0]
        idt = self.ident[0:kp, 0:kp]
        self.S.op("pe", lambda e: e.transpose(out, in_, idt), list(r) + [self.cbb], w)

    def act(self, out, in_, func, r, w, bias=None, scale=None, accum=None):
        kw = {}
        if bias is not None:
            kw["bias"] = bias
        if scale is not None:
            kw["scale"] = scale
        if accum is not None:
            kw["accum_out"] = accum
        self.S.op("act", lambda e: e.activation(out, in_, func, **kw), r, w)

    def amul(self, out, in_, mul, r, w):
        self.S.op("act", lambda e: e.mul(out, in_, mul), r, w)

    def tt(self, eng, out, in0, in1, op, r, w):
        self.S.op(eng, lambda e: e.tensor_tensor(out, in0, in1, op), r, w)

    def stt(self, eng, out, in0, scalar, in1, op0, op1, r, w):
        self.S.op(eng, lambda e: e.scalar_tensor_tensor(out, in0, scalar, in1, op0, op1), r, w)

    def ts(self, eng, out, in0, s1, s2, op0, op1, r, w):
        if s2 is None:
            self.S.op(eng, lambda e: e.tensor_scalar(out, in0, s1, None, op0), r, w)
        else:
            self.S.op(eng, lambda e: e.tensor_scalar(out, in0, s1, s2, op0, op1), r, w)

    def cp(self, eng, out, in_, r, w):
        if eng == "act":
            self.S.op("act", lambda e: e.copy(out, in_), r, w)
        else:
            self.S.op(eng, lambda e: e.tensor_copy(out, in_), r, w)

    def memset(self, eng, ap, val, w):
        self.S.op(eng, lambda e: e.memset(ap, val), (), w)

    def recip(self, out, in_, r, w):
        self.S.op("dve", lambda e: e.reciprocal(out, in_), r, w)

    def barrier(self):
        self.after = self.S.barrier()
        self.wslots = self.wslots[:2]
        self.wslotb = self.wslotb[:2]
        self.wpos = 0

    def wtile(self, wname, kc0, nkc, c0, ncols):
        i = self.wpos % len(self.wslots)
        self.wpos += 1
        slot, b = self.wslots[i], self.wslotb[i]
        src = self.wd[wname].rearrange("(c p) n -> p c n", p=128)[:, kc0:kc0 + nkc, c0:c0 + ncols]
        self.S.dma("sp", f"wr{i}", slot[:, 0:nkc, 0:ncols], src, reads=[self.wdb[wname]], writes=[b],
                   after=(self.after if i >= 2 else ()))
        return slot, b

    def extra_slots(self, scope, n):
        self.wslots = self.wslots[:2] + [self.sb(scope, f"wsx{i}", [128, 8, 512], BF16) for i in range(n)]
        self.wslotb = self.wslotb[:2] + [Buf(f"wsx{i}") for i in range(n)]
        self.wpos = 0

    def setup(self, dram):
        nc, es, S = self.nc, self.es, self.S
        self.dram = dram
        self.wd, self.wdb = {}, {}
        for name, K, N in WNAMES:
            self.wd[name] = nc.dram_tensor("wb_" + name, [K, N], BF16, kind="Internal").ap()
            self.wdb[name] = Buf("wd_" + name)
        order = ["gk2aug", "bgate", "gla_w_in", "gla_w_out", "ffn_w_up0", "ffn_w_down0", "ple_w_gate0",
                 "ple_w_proj0", "kv_w_in", "fox_w_in", "fox_w_out", "ffn_w_up1", "ffn_w_down1",
                 "ple_w_gate1", "ple_w_proj1"]
        dims = {n: (K, N) for n, K, N in WNAMES}
        self.wdb["gla_w_in_qk"] = Buf("wd_gla_w_in_qk")
        for (c0, c1) in ((0, 1024), (3072, 3088)):
            for r0 in range(0, 1024, 256):
                S.dma("pool", "cv_gla_qk", self.wd["gla_w_in"][r0:r0 + 256, c0:c1],
                      dram["gla_w_in"][r0:r0 + 256, c0:c1], writes=[self.wdb["gla_w_in_qk"]])
        for name in order:
            K, N = dims[name]
            if name == "gla_w_in":
                for r0 in range(0, 1024, 128):
                    S.dma("pool", "cv_" + name, self.wd[name][r0:r0 + 128, 1024:3072],
                          dram[name][r0:r0 + 128, 1024:3072], writes=[self.wdb[name]])
                continue
            r0 = 0
            while r0 < K:
                r1 = min(K, r0 + 128)
                S.dma("pool", "cv_" + name, self.wd[name][r0:r1, :], dram[name][r0:r1, :],
                      writes=[self.wdb[name]])
                r0 = r1
        self.cpp = self.sb(es, "cpp", [128, NCP], F32)
        self.roww = self.sb(es, "roww", [128, NRW], F32)
        self.cf32 = self.sb(es, "cf32", [128, 640], F32)
        self.cbf = self.sb(es, "cbf", [128, 256], BF16)
        self.bgate = self.sb(es, "bgate", [1, 2, 1024], BF16)
        self.ones = self.sb(es, "ones", [1, 128], BF16)
        self.cbb = Buf("consts")
        S.dma("sp", "cst", self.cpp[:], dram["cpp"], writes=[self.cbb])
        S.dma("sp", "cst", self.roww[:], dram["roww"].partition_broadcast(128), writes=[self.cbb])
        S.dma("sp", "cst", self.cf32[:], dram["cf32"], writes=[self.cbb])
        S.dma("sp", "cst", self.cbf[:], dram["cbf"], writes=[self.cbb])
        S.dma("sp", "cst", self.bgate[:], self.wd["bgate"].rearrange("(o l) n -> o l n", o=1),
              reads=[self.wdb["bgate"]], writes=[self.cbb])
        self.memset("dve", self.ones[:], 1.0, [self.cbb])
        self.ident = self.cbf[:, 0:128]
        self.maskneg = self.cbf[:, 128:256]
        self.mask01 = self.cf32[:, 0:128]
        self.mcum = self.cf32[:, 128:256]
        self.mrev = self.cf32[:, 256:384]
        self.trineg = self.cf32[:, 384:512]
        self.onesneg = self.cf32[:, 512:640]
        self.X = self.sb(es, "X", [128, NT, D], F32)
        self.Xb = [Buf(f"X{i}") for i in range(NT)]
        self.wslots = [self.sb(es, f"wslot{i}", [128, 8, 512], BF16) for i in range(2)]
        self.wslotb = [Buf(f"wslot{i}") for i in range(2)]
        self.wpos = 0
        self.pbF = self.sb(es, "pbF", [128, 4, 256], BF16)
        self.pbFb = Buf("pbF")
        self.nss2 = [self.sb(es, f"nss{i}", [128, 16], F32) for i in range(2)]
        self.nss2b = [Buf(f"nss{i}") for i in range(2)]
        self.sidx = 0
        self.nss, self.nssb = self.nss2[0], self.nss2b[0]
        self.nsq = self.sb(es, "nsq", [128, 16], F32); self.nsqb = Buf("nsq")
        self.nrs = self.sb(es, "nrs", [128, 16], F32); self.nrsb = Buf("nrs")
        self.xn2 = [self.sb(es, f"xn{i}", [128, 1024], BF16) for i in range(2)]
        self.xn2b = [Buf(f"xn{i}") for i in range(2)]
        self.junk, self.junkb = self.xn2[0], self.xn2b[0]

    def load_x(self, seq):
        self.cur_seq = seq
        self.stats_begin()
        if getattr(self, "preloaded", False):
            self.preloaded = False
        else:
            for i in range(NT):
                self.S.dma("sp", f"xl{i}", self.X[:, i, :], self.dram["x"][seq, i * 128:(i + 1) * 128, :],
                           writes=[self.Xb[i]])
        self.fresh_x = True

    def stats_begin(self):
        self.rd, self.rdb = self.nss, self.nssb
        self.sidx ^= 1
        self.nss, self.nssb = self.nss2[self.sidx], self.nss2b[self.sidx]
        self.memset("dve", self.nss[:], 0.0, [self.nssb])

    def x_final(self, ti):
        j = self.xn2[ti % 2]
        self.act(j[:], self.X[:, ti, :], AF.Square, [self.Xb[ti], self.nssb], [self.xn2b[ti % 2], self.nssb],
                 accum=self.nss[:, ti:ti + 1])

    def phase_rstd(self, t0=0, nt=NT, src=None):
        ss, ssb = src if src is not None else (self.nss, self.nssb)
        c = slice(t0, t0 + nt)
        self.act(self.nsq[:, c], ss[:, c], AF.Sqrt, [ssb], [self.nsqb], bias=EPS, scale=1.0 / D)
        self.recip(self.nrs[:, c], self.nsq[:, c], [self.nsqb], [self.nrsb])

    def norm_gen(self, t0, nt, gcol, HT, HTb):
        rs, rsb = self.nrs, self.nrsb
        htb = HTb if isinstance(HTb, list) else [HTb] * nt
        gbc = self.cpp[:, gcol:gcol + 8].unsqueeze(2).broadcast_to([128, 8, 128])
        for i in range(nt):
            xn, xnb = self.xn2[i % 2], self.xn2b[i % 2]
            self.amul(xn[:], self.X[:, t0 + i, :], rs[:, t0 + i:t0 + i + 1], [self.Xb[t0 + i], rsb], [xnb])
            pt, pb = self.ps(1)
            ptb = pt.bitcast(BF16)
            for kc in range(8):
                self.tr(ptb[:, kc * 128:(kc + 1) * 128], xn[:, kc * 128:(kc + 1) * 128], [xnb], pb)
            self.tt("dve", HT[:, :, i * 128:(i + 1) * 128], ptb.rearrange("p (c t) -> p c t", t=128), gbc,
                    ALU.mult, pb + [self.cbb], [htb[i]])
            yield i

    def norm_group(self, t0, nt, gcol, HT, HTb):
        for _ in self.norm_gen(t0, nt, gcol, HT, HTb):
            pass

    def gla(self):
        S = self.S
        fresh = getattr(self, "fresh_x", False)
        self.fresh_x = False
        if fresh:
            for i in range(8):
                self.x_final(i)
        self.stats_begin()
        NQ = 1040
        with ExitStack() as sc:
            self.extra_slots(sc, 2)
            win = self.sb(sc, "win", [128, 8, NQ], BF16); winb = Buf("win")
            wsrc = self.wd["gla_w_in"].rearrange("(c p) n -> p c n", p=128)
            S.dma("sp", "winl", win[:, :, 0:1024], wsrc[:, :, 0:1024], reads=[self.wdb["gla_w_in_qk"]],
                  writes=[winb], after=self.after)
            S.dma("sp", "winl", win[:, :, 1024:1040], wsrc[:, :, 3072:3088], reads=[self.wdb["gla_w_in_qk"]],
                  writes=[winb], after=self.after)
            gk2 = self.sb(sc, "gk2", [32, 512], BF16); gk2b = Buf("gk2")
            S.dma("sp", "gk2l", gk2[0:17, :], self.wd["gk2aug"], reads=[self.wdb["gk2aug"]], writes=[gk2b],
                  after=self.after)
            HT = self.sb(sc, "HT", [128, 8, 512], BF16); HTb = [Buf(f"HT{i}") for i in range(4)]
            AT, ATb = HT, HTb
            gr = self.sb(sc, "gr", [32, 512], BF16); grb = Buf("gr")
            sp = [self.sb(sc, f"sp{i}", [128, 512], F32) for i in range(2)]; spb = [Buf(), Buf()]
            eq = self.sb(sc, "eq", [128, 4, 512], F32); eqb = Buf("eq")
            ek = self.sb(sc, "ek", [128, 4, 512], F32); ekb = Buf("ek")
            een = [self.sb(sc, f"een{i}", [128, 512], F32) for i in range(2)]; eenb = [Buf(), Buf()]
            qin = self.sb(sc, "qin", [128, 4, 512], BF16); qinb = Buf("qin")
            kin = self.sb(sc, "kin", [128, 4, 512], BF16); kinb = Buf("kin")
            kend = [self.sb(sc, f"kend{i}", [128, 512], BF16) for i in range(4)]; kendb = [Buf() for _ in range(4)]
            vg = self.sb(sc, "vg", [128, 4, 1024], BF16); vgb = [Buf() for _ in range(4)]
            sgg = self.sb(sc, "sgg", [128, 4, 1024], BF16); sggb = [Buf() for _ in range(4)]
            attm = self.sb(sc, "attm", [128, 4, 128], BF16); attmb = Buf("attm")
            St = self.sb(sc, "St", [128, 4, 256], F32); Stb = Buf("St")
            Sbf = self.sb(sc, "Sbf", [128, 4, 256], BF16); Sbfb = Buf("Sbf")
            osq = self.sb(sc, "osq", [128, 4], F32); osqb = Buf("osq")
            ors = self.sb(sc, "ors", [128, 4], F32); orsb = Buf("ors")
            ojk = self.sb(sc, "ojk", [128, 256], BF16); ojkb = Buf("ojk")
            aa = [self.sb(sc, f"aa{i}", [128, 1024], BF16) for i in range(2)]; aab = [Buf(), Buf()]
            gonrow = self.sb(sc, "gonrow", [128, 1024], F32); gonrowb = Buf("gonrow")
            S.dma("sp", "gonl", gonrow[:], self.dram["gonrow"].partition_broadcast(128), writes=[gonrowb],
                  after=self.after)
            sgt, sgtb = een, eenb

            self.memset("dve", gr[:], 1.0, [grb])
            gonbc = self.cpp[:, CP_G["gon"]:CP_G["gon"] + 8].unsqueeze(2).broadcast_to([128, 8, 128])
            m01 = self.mask01.unsqueeze(1).broadcast_to([128, 4, 128])
            pend_final = []
            for g in range(4):
                self.phase_rstd(4 * g, 4, src=(self.rd, self.rdb))
                lazy = []
                if fresh and g + 2 < 4:
                    lazy = list(range(4 * (g + 2), 4 * (g + 3)))
                if False:
                    sv = (self.nss, self.nssb)
                    self.nss, self.nssb = self.rd, self.rdb
                    for i in range(4 * (g + 2), 4 * (g + 3)):
                        self.x_final(i)
                    self.nss, self.nssb = sv
                ngen = self.norm_gen(4 * g, 4, CP_G["g_mix0"], HT, HTb)
                next(ngen, None)
                for which in range(2):
                    for cg in range(2):
                        wt, wtb = self.wtile("gla_w_in", 0, 8, 1024 + which * 1024 + cg * 512, 512)
                        for i in range(4):
                            next(ngen, None)
                            tok = slice(i * 128, (i + 1) * 128)
                            pv, pvb = self.ps(1)
                            for kc in range(8):
                                self.mm(pv, HT[:, kc, tok], wt[:, kc, :], [wtb, HTb[i]], pvb,
                                        start=(kc == 0), stop=(kc == 7))
                            if which == 0:
                                self.cp("act", vg[:, i, cg * 512:(cg + 1) * 512], pv, pvb, [vgb[i]])
                            else:
                                k_ = (cg * 4 + i) % 2
                                self.act(sgt[k_][:], pv, AF.Silu, pvb, [sgtb[k_]])
                                self.tt("dve", sgg[:, i, cg * 512:(cg + 1) * 512], sgt[k_][:],
                                        gonrow[:, cg * 512:(cg + 1) * 512], ALU.mult, [sgtb[k_], gonrowb], [sggb[i]])
                        if which == 0 and cg == 0:
                            for ti_ in pend_final:
                                self.x_final(ti_)
                            pend_final = []
                            if lazy:
                                sv = (self.nss, self.nssb)
                                self.nss, self.nssb = self.rd, self.rdb
                                for ti_ in lazy:
                                    self.x_final(ti_)
                                self.nss, self.nssb = sv
                pg, pgb = self.ps(1)
                for kc in range(8):
                    self.mm(pg[0:16, :], win[:, kc, 1024:1040], HT[:, kc, :], [winb] + HTb, pgb,
                            start=(kc == 0), stop=(kc == 7))
                self.cp("dve", gr[0:16, :], pg[0:16, :], pgb, [grb])
                for i in range(4):
                    tok = slice(i * 128, (i + 1) * 128)
                    spt, sptb = sp[i % 2], spb[i % 2]
                    pz, pzb = self.ps(1)
                    self.mm(pz, gr[0:17, tok], gk2[0:17, :], [grb, gk2b], pzb)
                    self.act(spt[:], pz, AF.Exp, pzb, [sptb], scale=-1.0)
                    self.act(spt[:], spt[:], AF.Ln, [sptb], [sptb], bias=1.0)
                    pbt, pbtb = self.ps(1)
                    for h in range(4):
                        self.mm(pbt[:, h * 128:(h + 1) * 128], spt[:, h * 128:(h + 1) * 128], self.mcum,
                                [sptb, self.cbb], pbtb)
                    pb3 = pbt.rearrange("p (h c) -> p h c", c=128)
                    self.act(eq[:, :, tok], pb3, AF.Exp, pbtb, [eqb])
                    self.act(ek[:, :, tok], pb3, AF.Exp, pbtb, [ekb], scale=-1.0)
                    prv, prvb = self.ps(1)
                    self.mm(prv, self.mrev, spt[:], [sptb, self.cbb], prvb)
                    self.act(een[i % 2][:], prv, AF.Exp, prvb, [eenb[i % 2]])
                    pk2, pk2b = self.ps(1)
                    for kc in range(8):
                        self.mm(pk2, HT[:, kc, tok], win[:, kc, 512:1024], [winb, HTb[i]], pk2b,
                                start=(kc == 0), stop=(kc == 7))
                    self.tt("dve", kend[i][:], pk2, een[i % 2][:], ALU.mult, pk2b + [eenb[i % 2]], [kendb[i]])
                for h in range(4):
                    pq, pqb = self.ps(1)
                    for kc in range(8):
                        self.mm(pq, win[:, kc, h * 128:(h + 1) * 128], HT[:, kc, :], [winb] + HTb, pqb,
                                start=(kc == 0), stop=(kc == 7))
                    self.stt("dve", qin[:, h, :], pq, 128.0 ** -0.5, eq[:, h, :], ALU.mult, ALU.mult,
                             pqb + [eqb], [qinb])
                    pk, pkb = self.ps(1)
                    for kc in range(8):
                        self.mm(pk, win[:, kc, 512 + h * 128:512 + (h + 1) * 128], HT[:, kc, :], [winb] + HTb,
                                pkb, start=(kc == 0), stop=(kc == 7))
                    self.tt("dve", kin[:, h, :], pk, ek[:, h, :], ALU.mult, pkb + [ekb], [kinb])
                pendE = None

                def stageE(pe):
                    a_, ab_, tok_, ie_ = pe
                    pt, ptb_ = self.ps(1)
                    ptb = pt.bitcast(BF16)
                    for kc in range(8):
                        self.tr(ptb[:, kc * 128:(kc + 1) * 128], a_[:, kc * 128:(kc + 1) * 128], [ab_], ptb_)
                    self.cp("act", AT[:, :, tok_], ptb.rearrange("p (c t) -> p c t", t=128), ptb_, [ATb[ie_]])

                for i in range(4):
                    ti = 4 * g + i
                    tok = slice(i * 128, (i + 1) * 128)
                    pa, pab = self.ps(1)
                    for h in range(4):
                        self.mm(pa[:, h * 128:(h + 1) * 128], kin[:, h, tok], qin[:, h, tok], [kinb, qinb], pab)
                    self.tt("dve", attm[:], pa.rearrange("p (h c) -> p h c", c=128), m01, ALU.mult,
                            pab + [self.cbb], [attmb])
                    if ti < NT - 1:
                        pS, pSb = self.ps(2)
                        for h in range(4):
                            hs = slice(h * 256, (h + 1) * 256)
                            self.mm(pS[:, hs], kend[i][:, h * 128:(h + 1) * 128], vg[:, i, hs],
                                    [kendb[i], vgb[i]], pSb)
                    pO, pOb = self.ps(2)
                    for h in range(4):
                        hs = slice(h * 256, (h + 1) * 256)
                        self.mm(pO[:, hs], attm[:, h, :], vg[:, i, hs], [attmb, vgb[i]], pOb, start=True,
                                stop=(ti == 0))
                        if ti > 0:
                            self.mm(pO[:, hs], qin[:, h, tok], Sbf[:, h, :], [qinb, Sbfb], pOb,
                                    start=False, stop=True)
                    if ti < NT - 1:
                        for h in range(4):
                            hs = slice(h * 256, (h + 1) * 256)
                            if ti == 0:
                                self.cp("dve", St[:, h, :], pS[:, hs], pSb, [Stb])
                            else:
                                dcol = eq[:, h, i * 128 + 127:i * 128 + 128]
                                self.stt("dve", St[:, h, :], St[:, h, :], dcol, pS[:, hs], ALU.mult, ALU.add,
                                         pSb + [Stb, eqb], [Stb])
                        self.cp("act", Sbf[:], St[:], [Stb], [Sbfb])
                    self.memset("dve", osq[:], 0.0, [osqb])
                    for h in range(4):
                        hs = slice(h * 256, (h + 1) * 256)
                        self.act(ojk[:], pO[:, hs], AF.Square, pOb + [osqb], [ojkb, osqb], accum=osq[:, h:h + 1])
                    self.act(osq[:], osq[:], AF.Sqrt, [osqb], [osqb], bias=EPS, scale=1.0 / 256)
                    self.recip(ors[:], osq[:], [osqb], [orsb])
                    a, ab = aa[i % 2], aab[i % 2]
                    for h in range(4):
                        hs = slice(h * 256, (h + 1) * 256)
                        self.stt("dve", a[:, hs], pO[:, hs], ors[:, h:h + 1], sgg[:, i, hs], ALU.mult, ALU.mult,
                                 pOb + [orsb, sggb[i]], [ab])
                    if pendE is not None:
                        stageE(pendE)
                    pendE = (a, ab, tok, i)
                stageE(pendE)
                for cg in range(2):
                    wt, wtb = self.wtile("gla_w_out", 0, 8, cg * 512, 512)
                    for i in range(4):
                        ti = 4 * g + i
                        px, pxb = self.ps(1)
                        for kc in range(8):
                            self.mm(px, AT[:, kc, i * 128:(i + 1) * 128], wt[:, kc, :], [ATb[i], wtb], pxb,
                                    start=(kc == 0), stop=(kc == 7))
                        xs = self.X[:, ti, cg * 512:(cg + 1) * 512]
                        self.tt("dve", xs, xs, px, ALU.add, pxb + [self.Xb[ti]], [self.Xb[ti]])
                        if cg == 1:
                            pend_final.append(ti)
            for ti in pend_final:
                self.x_final(ti)
        self.barrier()

    def ffn(self, l):
        S = self.S
        G = 1024
        wn_up, wn_dn = f"ffn_w_up{l}", f"ffn_w_down{l}"
        gcol = CP_G[f"g_ffn{l}"]
        self.stats_begin()
        with ExitStack() as sc:
            self.extra_slots(sc, 2)
            HT = self.sb(sc, "HTf", [128, 8, G], BF16); HTb = Buf("HTf")
            AT = self.sb(sc, "ATf", [128, NFC, G], BF16); ATb = Buf("ATf")
            uur = [self.sb(sc, f"uu{i}", [128, G + 2], F32) for i in range(3)]; uurb = [Buf() for _ in range(3)]
            uurh = [Buf() for _ in range(3)]
            ccr = [self.sb(sc, f"cc{i}", [128, G], F32) for i in range(4)]; ccrb = [Buf() for _ in range(4)]
            rpos = 0

            def finish_pair(jp, resp):
                (cA, cAb), (cB, cBb) = resp
                self.act(cA[:], cA[:], AF.Silu, [cAb], [cAb])
                self.tt("dve", AT[:, jp, :], cA[:], cB[:], ALU.mult, [cAb, cBb], [ATb])
            hal = self.sb(sc, "hal", [128, 44, 2], F32); halb = Buf("hal")
            self.memset("pool", hal[:], 0.0, [halb])
            psrc = self.dram["p"][l, self.cur_seq, 0:512, :].rearrange("(i t) c -> t i c", t=128)
            S.dma("pool", "pldF", self.pbF[:], psrc, writes=[self.pbFb])
            ngen = None
            for g in range(T // G):
                if ngen is None:
                    self.phase_rstd(8 * g, 8, src=(self.rd, self.rdb))
                    self.norm_group(8 * g, 8, gcol, HT, HTb)
                else:
                    for _ in ngen:
                        pass
                ngen = None
                pend = None
                for j in range(NFC):
                    slot_i = self.wpos % len(self.wslots)
                    self.wpos += 1
                    wt, wtb = self.wslots[slot_i], self.wslotb[slot_i]
                    wsrc = self.wd[wn_up].rearrange("(c p) n -> p c n", p=128)
                    aft = self.after if slot_i >= 2 else ()
                    S.dma("sp", f"wr{slot_i}", wt[:, :, 0:128], wsrc[:, :, j * 128:(j + 1) * 128],
                          reads=[self.wdb[wn_up]], writes=[wtb], after=aft)
                    S.dma("sp", f"wr{slot_i}", wt[:, :, 128:256],
                          wsrc[:, :, DFF + j * 128:DFF + (j + 1) * 128], reads=[self.wdb[wn_up]], writes=[wtb],
                          after=aft)
                    res = []
                    for half, jj in enumerate([j, NFC + j]):
                        uu, uub, cc, ccb = uur[rpos % 3], uurb[rpos % 3], ccr[rpos % 4], ccrb[rpos % 4]
                        uuh = uurh[rpos % 3]
                        rpos += 1
                        self.cp("pool", uu[:, 0:2], hal[:, jj, :], [halb], [uuh])
                        for tg in range(G // 512):
                            pu, pub = self.ps(1)
                            for kc in range(8):
                                self.mm(pu, wt[:, kc, half * 128:(half + 1) * 128],
                                        HT[:, kc, tg * 512:(tg + 1) * 512], [wtb, HTb], pub,
                                        start=(kc == 0), stop=(kc == 7))
                            self.cp("act", uu[:, 2 + tg * 512:2 + (tg + 1) * 512], pu, pub, [uub])
                        self.cp("pool", hal[:, jj, :], uu[:, G:G + 2], [uub], [halb])
                        w0 = self.cpp[:, CP_WDW + (l * 3 + 0) * 44 + jj:CP_WDW + (l * 3 + 0) * 44 + jj + 1]
                        w1 = self.cpp[:, CP_WDW + (l * 3 + 1) * 44 + jj:CP_WDW + (l * 3 + 1) * 44 + jj + 1]
                        w2 = self.cpp[:, CP_WDW + (l * 3 + 2) * 44 + jj:CP_WDW + (l * 3 + 2) * 44 + jj + 1]
                        bb = self.cpp[:, CP_BDW + l * 44 + jj:CP_BDW + l * 44 + jj + 1]
                        self.act(cc[:], uu[:, 2:G + 2], AF.Identity, [uub, self.cbb], [ccb], bias=bb, scale=w2)
                        self.stt("dve", cc[:], uu[:, 1:G + 1], w1, cc[:], ALU.mult, ALU.add,
                                 [uub, uuh, ccb, self.cbb], [ccb])
                        self.stt("dve", cc[:], uu[:, 0:G], w0, cc[:], ALU.mult, ALU.add,
                                 [uub, uuh, ccb, self.cbb], [ccb])
                        res.append((cc, ccb))
                        if half == 0 and pend is not None:
                            finish_pair(*pend)
                    pend = (j, res)
                finish_pair(*pend)
                if g + 1 < T // G:
                    self.phase_rstd(8 * (g + 1), 8, src=(self.rd, self.rdb))
                    ngen = self.norm_gen(8 * (g + 1), 8, gcol, HT, HTb)
                for cg in range(2):
                    wts = []
                    for part, (k0, nk) in enumerate([(0, 8), (8, 8), (16, 6)]):
                        wts.append(self.wtile(wn_dn, k0, nk, cg * 512, 512) + (k0, nk))
                    for i in range(G // 128):
                        if ngen is not None and cg == 0:
                            next(ngen, None)
                        ti = 8 * g + i
                        px, pxb = self.ps(1)
                        for (wt, wtb, k0, nk) in wts:
                            for kk in range(nk):
                                j = k0 + kk
                                self.mm(px, AT[:, j, i * 128:(i + 1) * 128], wt[:, kk, :], [ATb, wtb], pxb,
                                        start=(j == 0), stop=(j == NFC - 1))
                        xs = self.X[:, ti, cg * 512:(cg + 1) * 512]
                        self.tt("dve", xs, xs, px, ALU.add, pxb + [self.Xb[ti]], [self.Xb[ti]])
                        if cg == 1:
                            self.x_final(ti)
        self.barrier()

    def ple(self, l, seq):
        S = self.S
        gcol = CP_G[f"g_ple{l}"]
        self.stats_begin()
        NG, NTG = 2, 8
        with ExitStack() as sc:
            self.extra_slots(sc, 2)
            HT = self.sb(sc, "HTp", [128, 8, NTG * 128], BF16); HTb = [Buf(f"HTp{i}") for i in range(NTG)]
            pb16s = [self.sb(sc, f"pb16_{i}", [128, NTG, 256], BF16) for i in range(NG)]
            pb16bs = [Buf(f"pb16_{i}") for i in range(NG)]
            for g in range(NG):
                i0_ = 4 if g == 0 else 0
                src = self.dram["p"][l, seq, (g * NTG + i0_) * 128:(g + 1) * NTG * 128, :].rearrange(
                    "(i t) c -> t i c", t=128)
                S.dma("pool", f"pld{g}", pb16s[g][:, i0_:NTG, :], src, writes=[pb16bs[g]], after=self.after)
            pT = self.sb(sc, "pT", [128, 2, NTG * 128], BF16); pTb = [Buf(f"pT{i}") for i in range(NTG)]
            sgt = [self.sb(sc, f"sgp{i}", [128, 512], F32) for i in range(2)]; sgtb = [Buf(), Buf()]
            tmp = [self.sb(sc, f"tmpp{i}", [128, 512], F32) for i in range(2)]; tmpb = [Buf(), Buf()]
            for g in range(NG):
                pb16, pb16b = pb16s[g], pb16bs[g]
                self.phase_rstd(NTG * g, NTG, src=(self.rd, self.rdb))
                ngen = self.norm_gen(NTG * g, NTG, gcol, HT, HTb)
                next(ngen, None)

                def ptrans(i):
                    pt, ptb_ = self.ps(1)
                    ptb = pt.bitcast(BF16)
                    pin, pinb = (self.pbF, self.pbFb) if (g == 0 and i < 4) else (pb16, pb16b)
                    for c in range(2):
                        self.tr(ptb[:, c * 128:(c + 1) * 128], pin[:, i, c * 128:(c + 1) * 128], [pinb], ptb_)
                    self.cp("dve", pT[:, :, i * 128:(i + 1) * 128],
                            ptb[:, 0:256].rearrange("p (c t) -> p c t", t=128), ptb_, [pTb[i]])

                ptrans(0)
                for cg in range(2):
                    wg, wgb = self.wtile(f"ple_w_gate{l}", 0, 8, cg * 512, 512)
                    wp, wpb = self.wtile(f"ple_w_proj{l}", 0, 2, cg * 512, 512)
                    for i in range(NTG):
                        if cg == 0:
                            next(ngen, None)
                            if i + 1 < NTG:
                                ptrans(i + 1)
                        ti = NTG * g + i
                        tok = slice(i * 128, (i + 1) * 128)
                        k = (cg * NTG + i) % 2
                        pg, pgb = self.ps(1)
                        for kc in range(8):
                            self.mm(pg, HT[:, kc, tok], wg[:, kc, :], [HTb[i], wgb], pgb, start=(kc == 0), stop=False)
                        self.mm(pg, self.ones[0:1, :], self.bgate[0:1, l, cg * 512:(cg + 1) * 512], [self.cbb], pgb,
                                start=False, stop=True)
                        self.act(sgt[k][:], pg, AF.Sigmoid, pgb, [sgtb[k]])
                        pp, ppb = self.ps(1)
                        for c in range(2):
                            self.mm(pp, pT[:, c, tok], wp[:, c, :], [pTb[i], wpb], ppb, start=(c == 0), stop=(c == 1))
                        self.tt("dve", tmp[k][:], pp, sgt[k][:], ALU.mult, ppb + [sgtb[k]], [tmpb[k]])
                        xs = self.X[:, ti, cg * 512:(cg + 1) * 512]
                        self.tt("dve", xs, xs, tmp[k][:], ALU.add, [tmpb[k], self.Xb[ti]], [self.Xb[ti]])
                        if cg == 1:
                            self.x_final(ti)
        self.barrier()

    def headnorm(self, ps_ap, psb, dst, gcol, qscale, tmp, tmpb, hsq, hsqb, hrs, hrsb):
        self.act(tmp[:], ps_ap, AF.Square, psb, [tmpb])
        self.S.op("dve", lambda e: e.tensor_reduce(hsq[:], tmp[:].rearrange("p (h d) -> p h d", d=64), AX.X,
                                                   ALU.add), [tmpb], [hsqb])
        self.act(hsq[:], hsq[:], AF.Sqrt, [hsqb], [hsqb], bias=EPS, scale=1.0 / 64)
        self.recip(hrs[:], hsq[:], [hsqb], [hrsb])
        p3 = ps_ap.rearrange("p (h d) -> p h d", d=64)
        t3 = tmp[:].rearrange("p (h d) -> p h d", d=64)
        self.stt("dve", t3, p3, qscale, hrs[:].unsqueeze(2).broadcast_to([128, 8, 64]), ALU.mult, ALU.mult,
                 psb + [hrsb, tmpb], [tmpb])
        gbc = self.roww[:, gcol:gcol + 64].unsqueeze(1).broadcast_to([128, 8, 64])
        return t3, gbc

    def fox(self):
        S = self.S
        HD = 70
        if getattr(self, "fresh_x", False):
            self.fresh_x = False
            for i in range(NT):
                self.x_final(i)
        self.stats_begin()
        with ExitStack() as sc:
            self.extra_slots(sc, 1)
            ka = self.sb(sc, "ka", [128, NT, 16, HD], BF16); kab = Buf("ka")
            va = self.sb(sc, "va", [128, NT, 16, 65], BF16); vab = Buf("va")
            wf = self.sb(sc, "wf", [128, 8, 16], BF16); wfb = Buf("wf")
            S.dma("sp", "wfl", wf[:], self.wd["kv_w_in"].rearrange("(c p) n -> p c n", p=128)[:, :, 2048:2064],
                  reads=[self.wdb["kv_w_in"]], writes=[wfb], after=self.after)
            HT = self.sb(sc, "HTx", [128, 8, 512], BF16); HTb = [Buf(f"HTx{i}") for i in range(4)]
            qa = self.sb(sc, "qa", [128, 4, 16, HD], BF16); qab = [Buf(f"qa{h}") for h in range(16)]
            csp = self.sb(sc, "csp", [128, 3, NT, 16], BF16); cspb = Buf("csp")
            tmp = [self.sb(sc, f"hn{i}", [128, 512], F32) for i in range(2)]; tmpb = [Buf(), Buf()]
            hsq2 = [self.sb(sc, f"hsq{i}", [128, 8], F32) for i in range(2)]; hsq2b = [Buf(), Buf()]
            hrs2 = [self.sb(sc, f"hrs{i}", [128, 8], F32) for i in range(2)]; hrs2b = [Buf(), Buf()]
            kvsc = ExitStack()
            spa = self.sb(kvsc, "spa", [128, NT, 16], F32); spab = Buf("spa")
            call = self.sb(kvsc, "call", [128, NT, 16], F32); callb = Buf("call")
            r1, r1b = spa, spab
            fsb = self.sb(kvsc, "fsb", [128, 16], F32); fsbb = Buf("fsb")
            self.memset("pool", va[:, :, :, 64:65], 1.0, [vab])
            self.memset("pool", ka[:, :, :, 67:70], 1.0, [kab])
            for h in range(16):
                self.memset("pool", qa[:, :, h, 64:67], 1.0, [qab[h]])
            for g in range(4):
                self.phase_rstd(4 * g, 4, src=(self.rd, self.rdb))
                ngen = self.norm_gen(4 * g, 4, CP_G["kv_g"], HT, HTb)
                next(ngen, None)
                for which in range(2):
                    for cg in range(2):
                        wt, wtb = self.wtile("kv_w_in", 0, 8, which * 1024 + cg * 512, 512)
                        for i in range(4):
                            next(ngen, None)
                            ti = 4 * g + i
                            tok = slice(i * 128, (i + 1) * 128)
                            pk, pkb = self.ps(1)
                            for kc in range(8):
                                self.mm(pk, HT[:, kc, tok], wt[:, kc, :], [wtb, HTb[i]], pkb,
                                        start=(kc == 0), stop=(kc == 7))
                            if which == 0:
                                k_ = (cg * 4 + i) % 2
                                t3, gbc = self.headnorm(pk, pkb, None, RW_GK, 1.0, tmp[k_], tmpb[k_], hsq2[k_],
                                                        hsq2b[k_], hrs2[k_], hrs2b[k_])
                                self.tt("dve", ka[:, ti, cg * 8:(cg + 1) * 8, 0:64], t3, gbc, ALU.mult,
                                        [tmpb[k_], self.cbb], [kab])
                            else:
                                self.cp("act", va[:, ti, cg * 8:(cg + 1) * 8, 0:64],
                                        pk.rearrange("p (h d) -> p h d", d=64), pkb, [vab])
                for i in range(4):
                    ti = 4 * g + i
                    tok = slice(i * 128, (i + 1) * 128)
                    pf, pfb = self.ps(1)
                    for kc in range(8):
                        self.mm(pf[:, 0:16], HT[:, kc, tok], wf[:, kc, :], [wfb, HTb[i]], pfb,
                                start=(kc == 0), stop=(kc == 7))
                    self.tt("dve", fsb[:], pf[:, 0:16], self.roww[:, RW_BF:RW_BF + 16], ALU.add,
                            pfb + [self.cbb], [fsbb])
                    self.act(fsb[:], fsb[:], AF.Exp, [fsbb], [fsbb], scale=-1.0)
                    self.act(spa[:, ti, :], fsb[:], AF.Ln, [fsbb], [spab], bias=1.0)
            for ti in range(NT):
                pc, pcb = self.ps(1)
                self.mm(pc[:, 0:16], self.trineg, spa[:, ti, :], [spab, self.cbb], pcb)
                if ti > 0:
                    self.mm(pc[:, 16:32], self.onesneg, spa[:, ti - 1, :], [spab, self.cbb], pcb)
                    if ti == 1:
                        self.cp("dve", fsb[:], pc[:, 16:32], pcb, [fsbb])
                    else:
                        self.tt("dve", fsb[:], fsb[:], pc[:, 16:32], ALU.add, pcb + [fsbb], [fsbb])
                    self.tt("dve", call[:, ti, :], pc[:, 0:16], fsb[:], ALU.add, pcb + [fsbb], [callb])
                else:
                    self.cp("dve", call[:, ti, :], pc[:, 0:16], pcb, [callb])
            self.cp("dve", csp[:, 0], call[:], [callb], [cspb])
            self.tt("dve", r1[:], call[:], csp[:, 0], ALU.subtract, [callb, cspb], [r1b])
            self.cp("dve", csp[:, 1], r1[:], [r1b], [cspb])
            self.tt("dve", r1[:], r1[:], csp[:, 1], ALU.subtract, [r1b, cspb], [r1b])
            self.cp("dve", csp[:, 2], r1[:], [r1b], [cspb])
            for k in range(3):
                self.ts("dve", ka[:, :, :, 64 + k], csp[:, k], -1.0, None, ALU.mult, None, [cspb], [kab])
            self.after = self.S.barrier()
            kvsc.close()
            kT = [self.sb(sc, f"kT{i}", [HD, T], BF16) for i in range(2)]; kTb = [Buf(), Buf()]
            qT = [self.sb(sc, f"qT{i}", [HD, 512], BF16) for i in range(2)]; qTb = [Buf(), Buf()]
            ptsall = self.sb(sc, "ptsall", [128, 3, 512], BF16)
            pts = [ptsall[:, i, :] for i in range(3)]; ptsb = [Buf() for _ in range(3)]
            rden = self.sb(sc, "rden", [128, 4], F32); rdenb = Buf("rden")
            a2v = ptsall[:, 0:2, :].rearrange("p a b -> p (a b)")
            a2bs = [ptsb[0], ptsb[1]]
            self.psn = 6
            self.pspos = 0
            pob = [self.psb[6], self.psb[7]]
            pot = [self.pst[3][:, 0:512], self.pst[3][:, 512:1024]]
            hcount = 0
            for g in range(4):
                ngen = self.norm_gen(4 * g, 4, CP_G["g_mix1"], HT, HTb)
                next(ngen, None)
                for cg in range(2):
                    wt, wtb = self.wtile("fox_w_in", 0, 8, cg * 512, 512)
                    for i in range(4):
                        next(ngen, None)
                        tok = slice(i * 128, (i + 1) * 128)
                        pq, pqb = self.ps(1)
                        for kc in range(8):
                            self.mm(pq, HT[:, kc, tok], wt[:, kc, :], [wtb, HTb[i]], pqb,
                                    start=(kc == 0), stop=(kc == 7))
                        k_ = (cg * 4 + i) % 2
                        t3, gbc = self.headnorm(pq, pqb, None, RW_GQ, 0.125, tmp[k_], tmpb[k_], hsq2[k_],
                                                hsq2b[k_], hrs2[k_], hrs2b[k_])
                        self.tt("dve", qa[:, i, cg * 8:(cg + 1) * 8, 0:64], t3, gbc, ALU.mult,
                                [tmpb[k_], self.cbb], qab[cg * 8:(cg + 1) * 8])
                for k in range(3):
                    self.cp("pool", qa[:, :, :, 67 + k], csp[:, k, 4 * g:4 * g + 4, :], [cspb], qab)
                nk = 4 * g + 4
                def prep(h, hb):
                    pt, ptb_ = self.ps(1)
                    ptb = pt.bitcast(BF16)
                    for i in range(4):
                        self.tr(ptb[0:HD, i * 128:(i + 1) * 128], qa[:, i, h, :], [qab[h]], ptb_)
                    self.cp("dve", qT[hb][:, :], ptb[0:HD, 0:512], ptb_, [qTb[hb]])
                    for j0 in range(0, nk, 4):
                        pt2, pt2b = self.ps(1)
                        pt2v = pt2.bitcast(BF16)
                        for jj in range(4):
                            self.tr(pt2v[0:HD, jj * 128:(jj + 1) * 128], ka[:, j0 + jj, h, :], [kab], pt2b)
                        self.cp("dve", kT[hb][:, j0 * 128:(j0 + 4) * 128], pt2v[0:HD, 0:512], pt2b, [kTb[hb]])

                def qk_exp(h, hb, j):
                    ks = slice(j * 128, (j + 1) * 128)
                    pS, pSb = self.ps(1)
                    i0 = max(0, j - 4 * g)
                    if j >= 4 * g:
                        dg = slice(i0 * 128, (i0 + 1) * 128)
                        self.mm(pS[:, dg], self.ident, self.maskneg, [self.cbb], pSb, start=True, stop=False)
                        self.mm(pS[:, dg], kT[hb][:, ks], qT[hb][:, dg], [kTb[hb], qTb[hb]], pSb,
                                start=False, stop=True)
                        if i0 < 3:
                            rest = slice((i0 + 1) * 128, 512)
                            self.mm(pS[:, rest], kT[hb][:, ks], qT[hb][:, rest], [kTb[hb], qTb[hb]], pSb)
                    else:
                        self.mm(pS, kT[hb][:, ks], qT[hb][:, :], [kTb[hb], qTb[hb]], pSb)
                    pp, ppb = pts[j % 3], ptsb[j % 3]
                    self.act(pp[:, i0 * 128:512], pS[:, i0 * 128:512], AF.Exp, pSb, [ppb])
                    return (j, i0, pp, ppb)

                def pv(h, hb, blk):
                    j, i0, pp, ppb = blk
                    po, pobuf = pot[hb], [pob[hb]]
                    for i in range(i0, 4):
                        self.mm(po[:, i * 128:i * 128 + 65], pp[:, i * 128:(i + 1) * 128], va[:, j, h, :],
                                [ppb, vab], pobuf, start=(j == 0 and i == 0), stop=(j == 4 * g + i))

                prep(0, hcount % 2)
                for h in range(16):
                    hb = hcount % 2
                    hcount += 1
                    if h + 1 < 16:
                        prep(h + 1, hcount % 2)
                    pendq = []
                    for j in range(nk):
                        pendq.append(qk_exp(h, hb, j))
                        if len(pendq) > 2:
                            pv(h, hb, pendq.pop(0))
                    for blk in pendq:
                        pv(h, hb, blk)
                    po = pot[hb]
                    pobuf = [pob[hb]]
                    po3 = po.rearrange("p (i c) -> p i c", c=128)
                    self.recip(rden[:].unsqueeze(2), po3[:, :, 64:65], pobuf, [rdenb])
                    self.tt("dve", qa[:, :, h, 0:64], po3[:, :, 0:64],
                            rden[:].unsqueeze(2).broadcast_to([128, 4, 64]), ALU.mult, pobuf + [rdenb], [qab[h]])
                wog = [self.wtile("fox_w_in", 0, 8, 1024 + cg * 512, 512) for cg in range(2)]
                for i in range(4):
                    tok = slice(i * 128, (i + 1) * 128)
                    for cg in range(2):
                        wt, wtb = wog[cg]
                        pg, pgb = self.ps(1)
                        for kc in range(8):
                            self.mm(pg, HT[:, kc, tok], wt[:, kc, :], [wtb, HTb[i]], pgb,
                                    start=(kc == 0), stop=(kc == 7))
                        k_ = (cg * 4 + i) % 2
                        self.act(tmp[k_][:], pg, AF.Sigmoid, pgb, [tmpb[k_]])
                        av = qa[:, i, cg * 8:(cg + 1) * 8, 0:64]
                        self.tt("dve", a2v[:, cg * 512:(cg + 1) * 512].rearrange("p (h d) -> p h d", d=64), av,
                                tmp[k_][:].rearrange("p (h d) -> p h d", d=64), ALU.mult,
                                [tmpb[k_]] + qab[cg * 8:(cg + 1) * 8], a2bs)
                    pt, ptb_ = self.ps(1)
                    ptb = pt.bitcast(BF16)
                    for kc in range(8):
                        self.tr(ptb[:, kc * 128:(kc + 1) * 128], a2v[:, kc * 128:(kc + 1) * 128], a2bs, ptb_)
                    self.cp("dve", HT[:, :, tok], ptb.rearrange("p (c t) -> p c t", t=128), ptb_, [HTb[i]])
                for cg in range(2):
                    wt, wtb = self.wtile("fox_w_out", 0, 8, cg * 512, 512)
                    for i in range(4):
                        ti = 4 * g + i
                        px, pxb = self.ps(1)
                        for kc in range(8):
                            self.mm(px, HT[:, kc, i * 128:(i + 1) * 128], wt[:, kc, :], [HTb[i], wtb], pxb,
                                    start=(kc == 0), stop=(kc == 7))
                        xs = self.X[:, ti, cg * 512:(cg + 1) * 512]
                        self.tt("dve", xs, xs, px, ALU.add, pxb + [self.Xb[ti]], [self.Xb[ti]])
                        if cg == 1:
                            self.x_final(ti)
            self.psn = 8
        self.barrier()

    def store_raw(self, seq, out):
        for i in range(NT):
            self.S.dma("sp", f"xs{i}", out[seq, i * 128:(i + 1) * 128, :], self.X[:, i, :], reads=[self.Xb[i]])

    def final_norm_store(self, seq, out, nseq=None):
        S = self.S
        ss, sq, rs = self.nss, self.nsq, self.nrs
        with ExitStack() as sc:
            NOB = 6
            ob = [self.sb(sc, f"ob{i}", [128, 1024], F32) for i in range(NOB)]; obb = [Buf() for _ in range(NOB)]
            gfin = self.sb(sc, "gfin", [128, 1024], F32); gfinb = Buf("gfin")
            S.dma("sp", "cst", gfin[:], self.dram["gfin"].partition_broadcast(128), writes=[gfinb], after=self.after)
            self.phase_rstd()
            for i in range(NT):
                o, obf = ob[i % NOB], obb[i % NOB]
                self.stt("dve", o[:], self.X[:, i, :], rs[:, i:i + 1], gfin[:],
                         ALU.mult, ALU.mult, [self.Xb[i], self.nrsb, gfinb], [obf])
                S.dma("act", f"ost{i % NOB}", out[seq, i * 128:(i + 1) * 128, :], o[:], reads=[obf], after=self.after)
                if nseq is not None:
                    S.dma("act", f"xl{i}", self.X[:, i, :], self.dram["x"][nseq, i * 128:(i + 1) * 128, :],
                          writes=[self.Xb[i]])
        self.barrier()

    def finish(self):
        S = self.S
        for key in list(S.sems.keys()):
            if key.startswith("ost") or key.startswith("xs"):
                S._need("sp", (key, S.cnt[key]), False)
        for e in ("pe", "act", "dve", "pool"):
            if S.cnt[e] > 0:
                S._need("sp", (e, S.cnt[e]), False)
        S.emit()


def build(nseq, layers=(0, 1), final=True):
    nc = bass.Bass("TRN2", target_bir_lowering=False)
    dram = {}
    dram["x"] = nc.dram_tensor("x", [nseq, T, D], F32, kind="ExternalInput").ap()
    dram["p"] = nc.dram_tensor("p", [2, nseq, T, 256], F32, kind="ExternalInput").ap()
    for name, K, N in WNAMES:
        dram[name] = nc.dram_tensor(name, [K, N], F32, kind="ExternalInput").ap()
    dram["cpp"] = nc.dram_tensor("cpp", [128, NCP], F32, kind="ExternalInput").ap()
    dram["roww"] = nc.dram_tensor("roww", [1, NRW], F32, kind="ExternalInput").ap()
    dram["gfin"] = nc.dram_tensor("gfin", [1, 1024], F32, kind="ExternalInput").ap()
    dram["gonrow"] = nc.dram_tensor("gonrow", [1, 1024], F32, kind="ExternalInput").ap()
    dram["cf32"] = nc.dram_tensor("cf32", [128, 640], F32, kind="ExternalInput").ap()
    dram["cbf"] = nc.dram_tensor("cbf", [128, 256], BF16, kind="ExternalInput").ap()
    out = nc.dram_tensor("out", [nseq, T, D], F32, kind="ExternalOutput").ap()
    with ExitStack() as es:
        kb = KB(nc, es, nseq)
        kb.setup(dram)
        for s in range(nseq):
            kb.load_x(s)
            if 0 in layers:
                kb.gla()
                kb.ffn(0)
                kb.ple(0, s)
            if 1 in layers:
                kb.fox()
                kb.ffn(1)
                kb.ple(1, s)
            if final:
                nseq_ = s + 1 if s + 1 < nseq else None
                kb.final_norm_store(s, out, nseq_)
                kb.preloaded = nseq_ is not None
            else:
                kb.store_raw(s, out)
        kb.finish()
    return nc


def run_layers(x, p, shared, layers, final, ncores=NCORES):
    B = x.shape[0]
    nseq = B // ncores
    nc = build(nseq, layers, final)
    in_maps = []
    for c in range(ncores):
        m = dict(shared)
        m["x"] = np.ascontiguousarray(x[c * nseq:(c + 1) * nseq])
        m["p"] = np.ascontiguousarray(p[:, c * nseq:(c + 1) * nseq])
        in_maps.append(m)
    res = run_bass_kernel_spmd(nc, in_maps, core_ids=list(range(ncores)))
    return np.concatenate([np.asarray(r["out"]) for r in res.results], axis=0)


MODE = "fused"


def kernel(**inputs):
    inp = {k: np.asarray(v) for k, v in inputs.items()}
    shared = _prep_shared(inp)
    x = np.ascontiguousarray(inp["x"], dtype=np.float32)
    p = np.ascontiguousarray(inp["p"], dtype=np.float32)
    if MODE == "fused":
        out = run_layers(x, p, shared, (0, 1), True)
    else:
        x1 = run_layers(x, p, shared, (0,), False)
        out = run_layers(x1, p, shared, (1,), True)
    return out.astype(np.float32)
```
